# Optimizing a Trainium2 kernel written in Bass

```python
import math
import jax
import jax.numpy as jnp
from jax import lax
import numpy as np

D_MODEL = 1024
BATCH = 2
SEQ = 8192
DEPTH = 4

GRID_W = 64
CTX_LEN = 256
N_MIXERS = 3
N_ATTN_LAYERS = (DEPTH + 2) // N_MIXERS
N_RET_LAYERS = (DEPTH + 1) // N_MIXERS
N_SSM_LAYERS = DEPTH // N_MIXERS
DA_HEADS = 8
DA_HEAD_DIM = D_MODEL // (2 * DA_HEADS)
QUERY_BLOCK = 128
RET_HEADS = 4
RET_KEY_DIM = D_MODEL // RET_HEADS
RET_VAL_DIM = 2 * D_MODEL // RET_HEADS
RET_CHUNK = 128
SSM_GROUP = 16
SSM_GROUPS = D_MODEL // SSM_GROUP
SSM_STATE = 64
DT_MIN = 0.001
DT_MAX = 0.1
FFN_HIDDEN = -(-8 * D_MODEL // (3 * 256)) * 256
ROPE_THETA = 10000.0
NORM_EPS = 1e-6

kernel_name = 'hybrid_diffattn_retention_s5_dit'


def rmsnorm(x, gain):
    xf = x.astype(jnp.float32)
    y = xf * lax.rsqrt(jnp.mean(xf * xf, axis=-1, keepdims=True) + NORM_EPS)
    return (y * gain.astype(jnp.float32)).astype(x.dtype)


def modulate(h, shift, scale):
    return h * (1.0 + scale) + shift


def axial_rope_tables(rows, head_dim):
    quarter = head_dim // 4
    freqs = ROPE_THETA ** (-jnp.arange(quarter, dtype=jnp.float32) / quarter)
    row = jnp.repeat(jnp.arange(rows, dtype=jnp.float32), GRID_W)
    col = jnp.tile(jnp.arange(GRID_W, dtype=jnp.float32), rows)
    ang_r = row[:, None] * freqs
    ang_c = col[:, None] * freqs
    ang = jnp.concatenate([ang_r, ang_r, ang_c, ang_c], axis=-1)
    return jnp.cos(ang), jnp.sin(ang)


def apply_rope(x, cos, sin):
    q4 = x.shape[-1] // 4
    xr = x.reshape(x.shape[:-1] + (2, 2, q4))
    rot = jnp.stack([-xr[..., 1, :], xr[..., 0, :]], axis=-2).reshape(x.shape)
    return x * cos.astype(x.dtype) + rot * sin.astype(x.dtype)


def swiglu(h, w_gate, w_up, w_down):
    return (jax.nn.silu(h @ w_gate) * (h @ w_up)) @ w_down


def diff_softmax_attend(q, k, v, lam):
    s = jnp.einsum('bhiqd,bhikd->bhiqk', q, k, preferred_element_type=jnp.float32)
    p = jax.nn.softmax(s, axis=-1)
    a = p[:, :, 0] - lam * p[:, :, 1]
    return jnp.einsum('bhqk,bhkv->bhqv', a.astype(v.dtype), v, preferred_element_type=jnp.float32)


def diff_attention(hc, hx, w_in, w_out, lam_vec, subln, lam_init, cos, sin, ctx_out):
    bsz, seq, _ = hx.shape
    scale = DA_HEAD_DIM ** -0.5

    def project(h):
        p = h @ w_in
        n = h.shape[1]
        q = p[..., :D_MODEL].reshape(bsz, n, DA_HEADS, 2, DA_HEAD_DIM).transpose(0, 2, 3, 1, 4)
        k = p[..., D_MODEL:2 * D_MODEL].reshape(bsz, n, DA_HEADS, 2, DA_HEAD_DIM).transpose(0, 2, 3, 1, 4)
        v = p[..., 2 * D_MODEL:].reshape(bsz, n, DA_HEADS, 2 * DA_HEAD_DIM).transpose(0, 2, 1, 3)
        return q * scale, k, v

    qc, kc, vc = project(hc)
    qx, kx, vx = project(hx)
    qx = apply_rope(qx, cos, sin)
    kx = apply_rope(kx, cos, sin)
    lv = lam_vec.astype(jnp.float32)
    lam = jnp.exp(jnp.sum(lv[0] * lv[1])) - jnp.exp(jnp.sum(lv[2] * lv[3])) + lam_init

    k_all = jnp.concatenate([kc, kx], axis=3)
    v_all = jnp.concatenate([vc, vx], axis=2)
    nb = seq // QUERY_BLOCK
    qb = jnp.moveaxis(qx.reshape(bsz, DA_HEADS, 2, nb, QUERY_BLOCK, DA_HEAD_DIM), 3, 0)
    ox = lax.map(lambda q: diff_softmax_attend(q, k_all, v_all, lam), qb)
    ox = jnp.moveaxis(ox, 0, 2).reshape(bsz, DA_HEADS, seq, 2 * DA_HEAD_DIM)

    def finish(o):
        n = o.shape[2]
        o = rmsnorm(o, subln) * (1.0 - lam_init)
        return o.transpose(0, 2, 1, 3).reshape(bsz, n, D_MODEL).astype(hx.dtype) @ w_out

    yx = finish(ox)
    yc = finish(diff_softmax_attend(qc, kc, vc, lam)) if ctx_out else None
    return yc, yx


def retention_chunked(q, k, v, log_g, r0):
    bsz, nh, t, _ = q.shape
    dv = v.shape[-1]
    n = t // RET_CHUNK
    idx = jnp.arange(RET_CHUNK, dtype=jnp.float32)
    diff = idx[:, None] - idx[None, :]
    inner_decay = jnp.where(diff >= 0, jnp.exp(log_g[:, None, None] * jnp.maximum(diff, 0.0)), 0.0)
    q_decay = jnp.exp(log_g[:, None] * (idx + 1.0))[..., None]
    k_decay = jnp.exp(log_g[:, None] * (RET_CHUNK - 1.0 - idx))[..., None]
    chunk_decay = jnp.exp(log_g * RET_CHUNK)[:, None, None]

    def to_chunks(z):
        return jnp.moveaxis(z.astype(jnp.float32).reshape(bsz, nh, n, RET_CHUNK, z.shape[-1]), 2, 0)

    def step(r, qkv):
        qc, kc, vc = qkv
        s = jnp.einsum('bhqd,bhkd->bhqk', qc, kc) * inner_decay
        inner = jnp.einsum('bhqk,bhkv->bhqv', s, vc)
        cross = jnp.einsum('bhqd,bhdv->bhqv', qc * q_decay, r)
        r_new = chunk_decay * r + jnp.einsum('bhkd,bhkv->bhdv', kc * k_decay, vc)
        return r_new, inner + cross

    r_fin, out = lax.scan(step, r0, (to_chunks(q), to_chunks(k), to_chunks(v)))
    return jnp.moveaxis(out, 0, 2).reshape(bsz, nh, t, dv), r_fin


def retention(hc, hx, w_in, w_out, decay_logit, cos, sin, ctx_out):
    bsz = hx.shape[0]

    def project(h):
        p = h @ w_in
        n = h.shape[1]
        def heads(z, dh):
            return z.reshape(bsz, n, RET_HEADS, dh).transpose(0, 2, 1, 3)
        q = heads(p[..., :D_MODEL], RET_KEY_DIM)
        k = heads(p[..., D_MODEL:2 * D_MODEL], RET_KEY_DIM) * (RET_KEY_DIM ** -0.5)
        v = heads(p[..., 2 * D_MODEL:4 * D_MODEL], RET_VAL_DIM)
        g = p[..., 4 * D_MODEL:]
        return q, k, v, g

    qc, kc, vc, gc = project(hc)
    qx, kx, vx, gx = project(hx)
    qx = apply_rope(qx, cos, sin)
    kx = apply_rope(kx, cos, sin)
    log_g = jax.nn.log_sigmoid(decay_logit.astype(jnp.float32))
    zero = jnp.zeros((bsz, RET_HEADS, RET_KEY_DIM, RET_VAL_DIM), jnp.float32)

    def flip(z):
        return jnp.flip(z, axis=2)

    oc_f, rc_f = retention_chunked(qc, kc, vc, log_g[0], zero)
    oc_b, rc_b = retention_chunked(flip(qc), flip(kc), flip(vc), log_g[1], zero)
    ox_f, _ = retention_chunked(qx, kx, vx, log_g[0], rc_f)
    ox_b, _ = retention_chunked(flip(qx), flip(kx), flip(vx), log_g[1], rc_b)

    def finish(o, g):
        n = o.shape[2]
        o = o * lax.rsqrt(jnp.mean(o * o, axis=-1, keepdims=True) + NORM_EPS)
        o = o.transpose(0, 2, 1, 3).reshape(bsz, n, RET_HEADS * RET_VAL_DIM).astype(g.dtype)
        return (jax.nn.silu(g) * o) @ w_out

    yx = finish(ox_f + flip(ox_b), gx)
    yc = finish(oc_f + flip(oc_b), gc) if ctx_out else None
    return yc, yx


def ssm_combine(e1, e2):
    a1, b1 = e1
    a2, b2 = e2
    return a1 * a2, a2 * b1 + b2


def ssm_direction(u, lam_bar, b_bar, c_mat, h0, reverse):
    start, end = (-1, 0) if reverse else (0, -1)
    bu = jnp.einsum('btgh,gph->btgp', u, b_bar)
    bu = bu.at[:, start].add(lam_bar * h0)
    a = jnp.broadcast_to(lam_bar, bu.shape)
    _, h = lax.associative_scan(ssm_combine, (a, bu), axis=1, reverse=reverse)
    y = jnp.einsum('btgp,ghp->btgh', h, c_mat).real
    return y, h[:, end]


def s5_mixer(hc, hx, lam_re, lam_im, log_dt, b_re, b_im, c_re, c_im, d_skip, w_glu, ctx_out):
    f32 = jnp.float32
    lam = lax.complex(lam_re.astype(f32), lam_im.astype(f32))
    lam_bar = jnp.exp(lam * jnp.exp(log_dt.astype(f32))[..., None])
    b_bar = ((lam_bar - 1.0) / lam)[..., None] * lax.complex(b_re.astype(f32), b_im.astype(f32))
    c_mat = lax.complex(c_re.astype(f32), c_im.astype(f32))

    def to_groups(h):
        return h.astype(f32).reshape(h.shape[0], h.shape[1], SSM_GROUPS, SSM_GROUP).astype(jnp.complex64)

    u_c, u_x = to_groups(hc), to_groups(hx)
    h0 = jnp.zeros((hx.shape[0], SSM_GROUPS, SSM_STATE), jnp.complex64)
    yc_f, st_f = ssm_direction(u_c, lam_bar[0], b_bar[0], c_mat[0], h0, False)
    yc_b, st_b = ssm_direction(u_c, lam_bar[1], b_bar[1], c_mat[1], h0, True)
    yx_f, _ = ssm_direction(u_x, lam_bar[0], b_bar[0], c_mat[0], st_f, False)
    yx_b, _ = ssm_direction(u_x, lam_bar[1], b_bar[1], c_mat[1], st_b, True)

    def finish(h, y):
        y = y.reshape(h.shape) + d_skip.astype(f32) * h.astype(f32)
        g = jax.nn.gelu(y).astype(h.dtype)
        return (g @ w_glu[:, :D_MODEL]) * jax.nn.sigmoid(g @ w_glu[:, D_MODEL:])

    yx = finish(hx, yx_f + yx_b)
    yc = finish(hc, yc_f + yc_b) if ctx_out else None
    return yc, yx


def setup_inputs(seed: int = 0) -> dict:
    key = jax.random.key(seed)
    ks = jax.random.split(key, 32)
    f32 = jnp.float32
    D = D_MODEL

    def nrm(k, shape, s):
        return jax.random.normal(k, shape, f32) * s

    gamma0 = 1.0 - 2.0 ** (-5.0 - np.arange(RET_HEADS))
    logit0 = jnp.asarray(np.log(gamma0 / (1.0 - gamma0)).astype(np.float32))
    return {
        'x': nrm(ks[0], (BATCH, SEQ, D), 1.0),
        'c': nrm(ks[1], (BATCH, D), 1.0),
        'ctx': nrm(ks[2], (BATCH, CTX_LEN, D), 1.0),
        'c_ctx': nrm(ks[3], (D,), 1.0),
        'mod_w': nrm(ks[4], (DEPTH, D, 6 * D), 0.5 * D ** -0.5),
        'mod_b': nrm(ks[5], (DEPTH, 6 * D), 0.02),
        'norm_g': 1.0 + nrm(ks[6], (DEPTH, 4, D), 0.02),
        'attn_w_in': nrm(ks[7], (N_ATTN_LAYERS, D, 3 * D), D ** -0.5),
        'attn_w_out': nrm(ks[8], (N_ATTN_LAYERS, D, D), D ** -0.5),
        'attn_lambda': nrm(ks[9], (N_ATTN_LAYERS, 4, DA_HEAD_DIM), 0.1),
        'attn_subln': 1.0 + nrm(ks[10], (N_ATTN_LAYERS, 2 * DA_HEAD_DIM), 0.02),
        'ret_w_in': nrm(ks[11], (N_RET_LAYERS, D, 6 * D), D ** -0.5),
        'ret_w_out': nrm(ks[12], (N_RET_LAYERS, 2 * D, D), (2 * D) ** -0.5),
        'ret_decay_logit': logit0 + nrm(ks[13], (N_RET_LAYERS, 2, RET_HEADS), 0.1),
        'ssm_lambda_re': -0.5 + nrm(ks[14], (N_SSM_LAYERS, 2, SSM_GROUPS, SSM_STATE), 0.01),
        'ssm_lambda_im': math.pi * jnp.arange(SSM_STATE, dtype=f32) + nrm(ks[15], (N_SSM_LAYERS, 2, SSM_GROUPS, SSM_STATE), 0.01),
        'ssm_log_dt': jax.random.uniform(ks[16], (N_SSM_LAYERS, 2, SSM_GROUPS), f32, math.log(DT_MIN), math.log(DT_MAX)),
        'ssm_b_re': nrm(ks[17], (N_SSM_LAYERS, 2, SSM_GROUPS, SSM_STATE, SSM_GROUP), (2 * SSM_GROUP) ** -0.5),
        'ssm_b_im': nrm(ks[18], (N_SSM_LAYERS, 2, SSM_GROUPS, SSM_STATE, SSM_GROUP), (2 * SSM_GROUP) ** -0.5),
        'ssm_c_re': nrm(ks[19], (N_SSM_LAYERS, 2, SSM_GROUPS, SSM_GROUP, SSM_STATE), SSM_STATE ** -0.5),
        'ssm_c_im': nrm(ks[20], (N_SSM_LAYERS, 2, SSM_GROUPS, SSM_GROUP, SSM_STATE), SSM_STATE ** -0.5),
        'ssm_d': nrm(ks[21], (N_SSM_LAYERS, D), 1.0),
        'ssm_w_glu': nrm(ks[22], (N_SSM_LAYERS, D, 2 * D), D ** -0.5),
        'ffn_w_gate': nrm(ks[23], (DEPTH, D, FFN_HIDDEN), D ** -0.5),
        'ffn_w_up': nrm(ks[24], (DEPTH, D, FFN_HIDDEN), D ** -0.5),
        'ffn_w_down': nrm(ks[25], (DEPTH, FFN_HIDDEN, D), FFN_HIDDEN ** -0.5),
    }


def reference(x, c, ctx, c_ctx, mod_w, mod_b, norm_g, attn_w_in, attn_w_out, attn_lambda, attn_subln,
              ret_w_in, ret_w_out, ret_decay_logit, ssm_lambda_re, ssm_lambda_im, ssm_log_dt,
              ssm_b_re, ssm_b_im, ssm_c_re, ssm_c_im, ssm_d, ssm_w_glu, ffn_w_gate, ffn_w_up, ffn_w_down):
    bsz, seq, _ = x.shape
    rows = seq // GRID_W
    cos_a, sin_a = axial_rope_tables(rows, DA_HEAD_DIM)
    cos_r, sin_r = axial_rope_tables(rows, RET_KEY_DIM)
    s_lat = jax.nn.silu(c)
    s_ctx = jax.nn.silu(c_ctx)
    h, hc = x, ctx
    for l in range(DEPTH):
        last = l == DEPTH - 1
        m = (s_lat @ mod_w[l] + mod_b[l]).reshape(bsz, 6, 1, D_MODEL)
        mc = (s_ctx @ mod_w[l] + mod_b[l]).reshape(6, 1, D_MODEL)
        ux = modulate(rmsnorm(h, norm_g[l, 0]), m[:, 0], m[:, 1])
        uc = modulate(rmsnorm(hc, norm_g[l, 0]), mc[0], mc[1])
        kind, j = l % N_MIXERS, l // N_MIXERS
        if kind == 0:
            lam_init = 0.8 - 0.6 * math.exp(-0.3 * l)
            yc, yx = diff_attention(uc, ux, attn_w_in[j], attn_w_out[j], attn_lambda[j], attn_subln[j],
                                    lam_init, cos_a, sin_a, not last)
        elif kind == 1:
            yc, yx = retention(uc, ux, ret_w_in[j], ret_w_out[j], ret_decay_logit[j], cos_r, sin_r, not last)
        else:
            yc, yx = s5_mixer(uc, ux, ssm_lambda_re[j], ssm_lambda_im[j], ssm_log_dt[j], ssm_b_re[j],
                              ssm_b_im[j], ssm_c_re[j], ssm_c_im[j], ssm_d[j], ssm_w_glu[j], not last)
        h = h + m[:, 2] * rmsnorm(yx, norm_g[l, 1])
        vx = modulate(rmsnorm(h, norm_g[l, 2]), m[:, 3], m[:, 4])
        h = h + m[:, 5] * rmsnorm(swiglu(vx, ffn_w_gate[l], ffn_w_up[l], ffn_w_down[l]), norm_g[l, 3])
        if not last:
            hc = hc + mc[2] * rmsnorm(yc, norm_g[l, 1])
            vc = modulate(rmsnorm(hc, norm_g[l, 2]), mc[3], mc[4])
            hc = hc + mc[5] * rmsnorm(swiglu(vc, ffn_w_gate[l], ffn_w_up[l], ffn_w_down[l]), norm_g[l, 3])
    return h
```

```python
import math
from contextlib import ExitStack
import numpy as np
import concourse.bass as bass
import concourse.mybir as mybir
from concourse.bass_utils import run_bass_kernel_spmd

F32 = mybir.dt.float32
BF16 = mybir.dt.bfloat16
AF = mybir.ActivationFunctionType
ALU = mybir.AluOpType
AX = mybir.AxisListType

ENGS = ("pe", "act", "dve", "pool", "sp")
N_DMA_SEMS = 8


class Op:
    __slots__ = ("eng", "fn", "deps", "is_dma", "idx", "seq", "signal", "dma_slot", "dma_val")

    def __init__(self, eng, fn, is_dma):
        self.eng, self.fn, self.is_dma = eng, fn, is_dma
        self.deps = set()
        self.signal = False
        self.seq = None
        self.dma_slot = None
        self.dma_val = None


class Prog:
    def __init__(self, nc):
        self.nc = nc
        self.ops = []
        self.last_w = {}
        self.readers = {}

    def add(self, eng, fn, reads=(), writes=(), dma=False):
        op = Op(eng, fn, dma)
        op.idx = len(self.ops)
        for t in reads:
            w = self.last_w.get(t)
            if w is not None:
                op.deps.add(w)
        for t in writes:
            w = self.last_w.get(t)
            if w is not None:
                op.deps.add(w)
            for r in self.readers.get(t, ()):
                op.deps.add(r)
        for t in reads:
            self.readers.setdefault(t, []).append(op.idx)
        for t in writes:
            self.last_w[t] = op.idx
            self.readers[t] = []
        op.deps.discard(op.idx)
        self.ops.append(op)
        return op.idx

    def pe(self, fn, reads=(), writes=()):
        return self.add("pe", fn, reads, writes)

    def act(self, fn, reads=(), writes=()):
        return self.add("act", fn, reads, writes)

    def dve(self, fn, reads=(), writes=()):
        return self.add("dve", fn, reads, writes)

    def dma(self, eng, fn, reads=(), writes=()):
        return self.add(eng, fn, reads, writes, dma=True)

    def emit(self, final_wait_ops=()):
        nc = self.nc
        ops = self.ops
        for op in ops:
            for d in op.deps:
                dop = ops[d]
                if dop.is_dma or dop.eng != op.eng or dop.eng != "pe":
                    dop.signal = True
        for d in final_wait_ops:
            ops[d].signal = True
        seq = {e: 0 for e in ENGS}
        dma_cnt = {e: 0 for e in ENGS}
        for op in ops:
            if op.is_dma:
                i = dma_cnt[op.eng]
                dma_cnt[op.eng] += 1
                op.dma_slot = i % N_DMA_SEMS
                op.dma_val = 16 * (i // N_DMA_SEMS + 1)
                op.seq = i
            elif op.signal:
                seq[op.eng] += 1
                op.seq = seq[op.eng]
        with ExitStack() as st:
            esem = {e: st.enter_context(nc.semaphore("s_" + e)) for e in ENGS}
            dsem = {e: [st.enter_context(nc.semaphore("d_%s%d" % (e, k))) for k in range(N_DMA_SEMS)]
                    for e in ENGS if dma_cnt[e] > 0}
            st.enter_context(nc.allow_non_contiguous_dma(reason="small parameter layouts"))
            block = st.enter_context(nc.Block())
            per_eng = {e: [op for op in ops if op.eng == e] for e in ENGS}

            def run_stream(ename, engine):
                waited = {}
                for op in per_eng[ename]:
                    need = {}
                    for d in op.deps:
                        dop = ops[d]
                        if dop.is_dma:
                            key = ("d", dop.eng, dop.dma_slot)
                            need[key] = max(need.get(key, 0), dop.dma_val)
                        else:
                            if dop.eng == ename and ename == "pe":
                                continue
                            key = ("e", dop.eng)
                            need[key] = max(need.get(key, 0), dop.seq)
                    if op.is_dma and op.seq >= N_DMA_SEMS:
                        key = ("d", ename, op.dma_slot)
                        need[key] = max(need.get(key, 0), op.dma_val - 16)
                    for key, val in need.items():
                        if waited.get(key, 0) >= val:
                            continue
                        waited[key] = val
                        sem = esem[key[1]] if key[0] == "e" else dsem[key[1]][key[2]]
                        engine.wait_ge(sem, val)
                    ins = op.fn(engine)
                    if op.is_dma:
                        ins.then_inc(dsem[ename][op.dma_slot], 16)
                    elif op.signal:
                        ins.then_inc(esem[ename], 1)
                if ename == "sp":
                    for d in final_wait_ops:
                        dop = ops[d]
                        if dop.is_dma:
                            engine.wait_ge(dsem[dop.eng][dop.dma_slot], dop.dma_val)
                        else:
                            engine.wait_ge(esem[dop.eng], dop.seq)

            @block.tensor
            def _(e):
                run_stream("pe", e)

            @block.scalar
            def _(e):
                run_stream("act", e)

            @block.vector
            def _(e):
                run_stream("dve", e)

            @block.gpsimd
            def _(e):
                run_stream("pool", e)

            @block.sync
            def _(e):
                run_stream("sp", e)


D = 1024
KT = 8
NT = 2304
NTT = 18
CTX = 256
LAT = 2048
SEQ = 8192
TALL = CTX + SEQ
NFT = 22
DEPTH = 4
EPS = 1e-6
TWO_PI = 2.0 * math.pi
MAGIC = 12582912.0
PI_LIM = math.pi * (1.0 - 1e-6)
ALL8 = [list(range(8))]


class Base:
    def __init__(self):
        self.nc = bass.Bass("TRN2", target_bir_lowering=False)
        self.P = Prog(self.nc)
        self.st = ExitStack()
        self.slot_rr = 0
        self.outs = []

    def sb(self, name, shape, dt):
        return self.st.enter_context(self.nc.sbuf_tensor(name, shape, dt))

    def din(self, name, shape, dt=F32):
        return self.nc.dram_tensor(name, shape, dt, kind="ExternalInput").ap()

    def dout(self, name, shape, dt=F32):
        return self.nc.dram_tensor(name, shape, dt, kind="ExternalOutput").ap()

    def bank(self, i):
        return ("ps", i)

    def psum(self):
        self.PS = [self.st.enter_context(self.nc.psum_tensor("ps%d" % i, [128, 512], F32)) for i in range(8)]

    def gather_weight(self, name, rows, cols, flat_cols=None):
        P, nc = self.P, self.nc
        src = self.din(name, [rows, cols])
        bf = nc.dram_tensor(name + "_bf", [rows, cols], BF16)
        nch = max(1, rows // 256)
        rpc = rows // nch
        for c in range(nch):
            lo, hi = c * rpc, (rows if c == nch - 1 else (c + 1) * rpc)
            P.dma("pool", lambda e, lo=lo, hi=hi: e.dma_start(out=bf.ap()[lo:hi, :], in_=src[lo:hi, :]), writes=[("wb", name)])
        return bf.ap()

    def load_slot(self, src_ap, shape_view, wtok):
        i = self.slot_rr % len(self.SLOTS)
        self.slot_rr += 1
        n = 1
        for s in shape_view[1:]:
            n *= s
        view = self.SLOTS[i][:, 0:n].rearrange("p (a b) -> p a b", a=shape_view[1])
        tok = ("slot", i)
        self.P.dma("sp", lambda e, v=view, s=src_ap: e.dma_start(out=v, in_=s), reads=[wtok], writes=[tok])
        return view, tok

    def range_reduce(self, X, xt, tmp, tmptok, w):
        P = self.P
        P.dve(lambda e: e.tensor_scalar(out=tmp[:, 0:w], in0=X[:, 0:w], scalar1=1.0 / TWO_PI, scalar2=MAGIC, op0=ALU.mult, op1=ALU.add),
              reads=[xt], writes=[tmptok])
        P.dve(lambda e: e.tensor_scalar(out=tmp[:, 0:w], in0=tmp[:, 0:w], scalar1=MAGIC, scalar2=-TWO_PI, op0=ALU.subtract, op1=ALU.mult),
              reads=[tmptok], writes=[tmptok])
        P.dve(lambda e: e.tensor_tensor(out=X[:, 0:w], in0=X[:, 0:w], in1=tmp[:, 0:w], op=ALU.add), reads=[xt, tmptok], writes=[xt])
        P.dve(lambda e: e.tensor_scalar(out=X[:, 0:w], in0=X[:, 0:w], scalar1=-PI_LIM, scalar2=PI_LIM, op0=ALU.max, op1=ALU.min),
              reads=[xt], writes=[xt])

    def finish(self):
        self.P.emit(final_wait_ops=self.outs)


class TokProg(Base):
    def __init__(self, mode, ncols=0, kdim=0, stop=None):
        super().__init__()
        self.mode, self.ncols, self.kdim, self.stop = mode, ncols, kdim, stop
        with self.st:
            self.build()

    def build(self):
        P, nc = self.P, self.nc
        mode = self.mode
        self.h_loc = self.din("h_loc", [NT, D])
        self.cvec = self.din("cvec", [2, D])
        self.mod_b = self.din("mod_b", [1, 6 * D])
        self.norm_g = self.din("norm_g", [4, D])
        self.cmat = self.din("cmat", [128, 3, 128])
        self.mw = self.gather_weight("mod_w", D, 6 * D)
        if mode == "A":
            self.w_in = self.gather_weight("w_in", D, self.ncols)
            self.p_out = self.dout("p_out", [NT, self.ncols])
        elif mode == "A_s5":
            self.ut_out = self.dout("ut_out", [D, NT])
        else:
            if mode == "C":
                self.ot_in = self.din("ot_in", [self.kdim, NT])
                self.w_out = self.gather_weight("w_out", self.kdim, D)
            else:
                self.yf_in = self.din("yf_in", [D, NT])
                self.yb_in = self.din("yb_in", [D, NT])
                self.ut_in = self.din("ut_in", [D, NT])
                self.ssm_d = self.din("ssm_d", [1, D])
                self.w_glu = self.gather_weight("w_glu", D, 2 * D)
            self.wg = self.gather_weight("ffn_g", D, 2816, flat_cols=1024)
            self.wu = self.gather_weight("ffn_u", D, 2816, flat_cols=1024)
            self.wd = self.gather_weight("ffn_d", 2816, D)
            self.h_out = self.dout("h_out", [NT, D])
        sb = self.sb
        self.H = sb("H", [128, NTT, D], F32)
        self.UT = sb("UT", [128, KT, NT], BF16)
        self.SLOTS = [sb("slot%d" % i, [128, 4096], BF16) for i in range(3)]
        self.GROW = sb("GROW", [128, 2, D], F32)
        self.TMPROW = sb("TMPROW", [128, D], F32)
        self.XN = [sb("XN%d" % i, [128, D], BF16) for i in range(2)]
        self.SCR = sb("SCR", [128, D], BF16)
        self.SREP = sb("SREP", [128, 2, KT, 128], BF16)
        self.CM = sb("CM", [128, 3, 128], BF16)
        self.SCOL = sb("SCOL", [128, KT, 2], BF16)
        self.CRAW = sb("CRAW", [128, 2, KT], F32)
        self.MCOL = sb("MCOL", [128, 48, 2], F32)
        self.MBC = sb("MBC", [128, 48], F32)
        self.NGC = sb("NGC", [128, 4, KT], F32)
        self.AB = sb("AB", [128, 2, 2, KT], F32)
        self.ST = sb("ST", [128, 64], F32)
        self.T4 = [sb("T4_%d" % i, [128, 512], F32) for i in range(4)]
        self.psum()
        xv = self.h_loc.rearrange("(t p) d -> p t d", p=128)
        for t0 in range(0, NTT, 3):
            P.dma("sp", (lambda e, t0=t0: e.dma_start(out=self.H[:, t0:t0 + 3, :], in_=xv[:, t0:t0 + 3, :])),
                  writes=[("H", t) for t in range(t0, t0 + 3)])
        P.dma("pool", lambda e: e.dma_start(out=self.CM[:], in_=self.cmat), writes=["CM"])
        P.dma("sp", lambda e: e.dma_start(out=self.CRAW[:], in_=self.cvec.rearrange("s (k p) -> p s k", p=128)), writes=["CRAW"])
        P.dma("sp", lambda e: e.dma_start(out=self.MBC[:], in_=self.mod_b.rearrange("o (c p) -> p (o c)", p=128)), writes=["MBC"])
        P.dma("sp", lambda e: e.dma_start(out=self.NGC[:], in_=self.norm_g.rearrange("n (k p) -> p n k", p=128)), writes=["NGC"])
        P.act(lambda e: e.activation(out=self.CRAW[:], in_=self.CRAW[:], func=AF.Silu), reads=["CRAW"], writes=["CRAW"])
        for s in range(2):
            P.dve(lambda e, s=s: e.tensor_copy(out=self.SCOL[:, :, s], in_=self.CRAW[:, s, :]), reads=["CRAW"], writes=["SCOL"])
            for k in range(KT):
                P.dve(lambda e, s=s, k=k: e.tensor_copy(out=self.SREP[:, s, k, :], in_=self.CRAW[:, s, k:k + 1].to_broadcast([128, 128])),
                      reads=["CRAW"], writes=["SREP"])
        self.mod_columns()
        all_tiles = list(range(NTT))
        if mode == "A":
            self.norm_to_UT(0, all_tiles)
            self.in_proj()
        elif mode == "A_s5":
            self.norm_to_UT(0, all_tiles)
            USTG = [self.sb("USTG%d" % i, [128, NT], F32) for i in range(2)]
            for k in range(KT):
                P.act(lambda e, k=k: e.activation(out=USTG[k % 2][:], in_=self.UT[:, k, :], func=AF.Copy),
                      reads=[("UT", t) for t in range(NTT)], writes=[("USTG", k % 2)])
                self.outs.append(P.dma("sp", lambda e, k=k: e.dma_start(out=self.ut_out[k * 128:(k + 1) * 128, :], in_=USTG[k % 2][:]),
                                       reads=[("USTG", k % 2)]))
        else:
            stop = self.stop
            self.gate_rows(0)
            if stop != "gate":
                if mode == "C":
                    self.out_proj()
                else:
                    self.s5_finish()
                if stop != "outproj":
                    self.norm_to_UT(1, all_tiles)
                    self.gate_rows(1)
                    if stop != "norm2":
                        self.ffn()
            ov = self.h_out.rearrange("(t p) d -> p t d", p=128)
            for t0 in range(0, NTT, 3):
                self.outs.append(P.dma("sp", lambda e, t0=t0: e.dma_start(out=ov[:, t0:t0 + 3, :], in_=self.H[:, t0:t0 + 3, :]),
                                       reads=[("H", t) for t in range(t0, t0 + 3)]))
        self.finish()

    def mod_columns(self):
        P = self.P
        wsrc = self.mw.rearrange("(k p) n -> p k n", p=128)
        for g in range(12):
            view, tok = self.load_slot(wsrc[:, :, g * 512:(g + 1) * 512], [128, KT, 512], ("wb", "mod_w"))
            bk = g % 2
            for c4 in range(4):
                for k in range(KT):
                    P.pe(lambda e, v=view, c4=c4, k=k, bk=bk: e.matmul(
                        self.PS[bk][:, c4 * 2:c4 * 2 + 2], v[:, k, c4 * 128:(c4 + 1) * 128], self.SCOL[:, k, :],
                        start=(k == 0), stop=(k == KT - 1)), reads=[tok, "SCOL"], writes=[self.bank(bk)])
            P.dve(lambda e, g=g, bk=bk: e.tensor_tensor(
                out=self.MCOL[:, g * 4:(g + 1) * 4, :], in0=self.PS[bk][:, 0:8].rearrange("p (c s) -> p c s", s=2),
                in1=self.MBC[:, g * 4:(g + 1) * 4].unsqueeze(2).to_broadcast([128, 4, 2]), op=ALU.add),
                reads=[self.bank(bk), "MBC"], writes=["MCOL"])

    def gate_rows(self, which):
        P = self.P
        chunk = 2 if which == 0 else 5
        gidx = 1 if which == 0 else 3
        wsrc = self.mw.rearrange("(k p) n -> p k n", p=128)
        P.dma("sp", lambda e: e.dma_start(out=self.TMPROW[:], in_=self.mod_b[0:1, chunk * 1024:(chunk + 1) * 1024].broadcast_to([128, 1024])),
              writes=["TMPROW"])
        for hf in range(2):
            view, tok = self.load_slot(wsrc[:, :, chunk * 1024 + hf * 512: chunk * 1024 + (hf + 1) * 512], [128, KT, 512], ("wb", "mod_w"))
            for s in range(2):
                bk = hf * 2 + s
                for k in range(KT):
                    P.pe(lambda e, v=view, k=k, s=s, bk=bk: e.matmul(self.PS[bk][:, :], self.SREP[:, s, k, :], v[:, k, :],
                                                                   start=(k == 0), stop=(k == KT - 1)),
                         reads=[tok, "SREP"], writes=[self.bank(bk)])
                P.dve(lambda e, s=s, hf=hf, bk=bk: e.tensor_tensor(out=self.GROW[:, s, hf * 512:(hf + 1) * 512], in0=self.PS[bk][:, :],
                                                                 in1=self.TMPROW[:, hf * 512:(hf + 1) * 512], op=ALU.add),
                      reads=[self.bank(bk), "TMPROW"], writes=[("GROW", s)])
        P.dma("sp", lambda e: e.dma_start(out=self.TMPROW[:], in_=self.norm_g[gidx:gidx + 1, :].broadcast_to([128, 1024])), writes=["TMPROW"])
        for s in range(2):
            P.dve(lambda e, s=s: e.scalar_tensor_tensor(out=self.GROW[:, s, :], in0=self.GROW[:, s, :], scalar=32.0, in1=self.TMPROW[:],
                                                        op0=ALU.mult, op1=ALU.mult),
                  reads=["TMPROW", ("GROW", s)], writes=[("GROW", s)])

    def norm_to_UT(self, which, tiles):
        P = self.P
        gidx = 0 if which == 0 else 2
        sh, sc = (0, 1) if which == 0 else (3, 4)
        for s in range(2):
            P.dve(lambda e, s=s: e.scalar_tensor_tensor(out=self.AB[:, 0, s, :], in0=self.MCOL[:, sc * 8:(sc + 1) * 8, s], scalar=1.0,
                                                        in1=self.NGC[:, gidx, :], op0=ALU.add, op1=ALU.mult),
                  reads=["MCOL", "NGC"], writes=["AB"])
            P.dve(lambda e, s=s: e.tensor_copy(out=self.AB[:, 1, s, :], in_=self.MCOL[:, sh * 8:(sh + 1) * 8, s]), reads=["MCOL"], writes=["AB"])
        for t in tiles:
            s = 1 if t < 2 else 0
            xn = self.XN[t % 2]
            xtok = ("XN", t % 2)
            c0 = (t % 8) * 4
            P.act(lambda e, t=t, c0=c0: e.activation(out=self.SCR[:, :], in_=self.H[:, t, :], func=AF.Square, accum_out=self.ST[:, c0:c0 + 1]),
                  reads=[("H", t)], writes=["SCR", ("ST", c0)])
            P.act(lambda e, c0=c0: e.activation(out=self.ST[:, c0 + 1:c0 + 2], in_=self.ST[:, c0:c0 + 1], func=AF.Sqrt, scale=1.0 / D, bias=EPS),
                  reads=[("ST", c0)], writes=[("ST", c0 + 1)])
            P.dve(lambda e, c0=c0: e.reciprocal(out=self.ST[:, c0 + 2:c0 + 3], in_=self.ST[:, c0 + 1:c0 + 2]),
                  reads=[("ST", c0 + 1)], writes=[("ST", c0 + 2)])
            P.act(lambda e, t=t, xn=xn, c0=c0: e.activation(out=xn[:, :], in_=self.H[:, t, :], func=AF.Copy, scale=self.ST[:, c0 + 2:c0 + 3]),
                  reads=[("H", t), ("ST", c0 + 2)], writes=[xtok])
            bk = 6 + (t % 2)
            psb = self.PS[bk][:, :].bitcast(BF16)
            for k in range(KT):
                P.pe(lambda e, xn=xn, k=k, psb=psb: e.transpose(psb[:, k * 128:(k + 1) * 128], xn[:, k * 128:(k + 1) * 128], self.CM[:, 0, :]),
                     reads=[xtok, "CM"], writes=[self.bank(bk)])
            for k in range(KT):
                if k % 2 == 0:
                    P.act(lambda e, k=k, t=t, s=s, psb=psb: e.activation(
                        out=self.UT[:, k, t * 128:(t + 1) * 128], in_=psb[:, k * 128:(k + 1) * 128], func=AF.Identity,
                        scale=self.AB[:, 0, s, k:k + 1], bias=self.AB[:, 1, s, k:k + 1]), reads=[self.bank(bk), "AB"], writes=[("UT", t)])
                else:
                    P.dve(lambda e, k=k, t=t, s=s, psb=psb: e.tensor_scalar(
                        out=self.UT[:, k, t * 128:(t + 1) * 128], in0=psb[:, k * 128:(k + 1) * 128],
                        scalar1=self.AB[:, 0, s, k:k + 1], scalar2=self.AB[:, 1, s, k:k + 1], op0=ALU.mult, op1=ALU.add),
                        reads=[self.bank(bk), "AB"], writes=[("UT", t)])

    def gated_residual(self, t, srcs, toks):
        P = self.P
        s = 1 if t < 2 else 0
        c0 = 32 + (t % 4) * 4
        for hf in range(2):
            P.act(lambda e, hf=hf, c0=c0: e.activation(out=self.SCR[:, 0:512], in_=srcs[hf], func=AF.Square,
                                                       accum_out=self.ST[:, c0 + hf:c0 + hf + 1]),
                  reads=[toks[hf]], writes=["SCR", ("ST", c0 + hf)])
        P.dve(lambda e, c0=c0: e.tensor_scalar(out=self.ST[:, c0 + 2:c0 + 3], in0=self.ST[:, c0:c0 + 1], scalar1=self.ST[:, c0 + 1:c0 + 2],
                                               scalar2=D * EPS, op0=ALU.add, op1=ALU.add),
              reads=[("ST", c0), ("ST", c0 + 1)], writes=[("ST", c0 + 2)])
        P.act(lambda e, c0=c0: e.activation(out=self.ST[:, c0 + 2:c0 + 3], in_=self.ST[:, c0 + 2:c0 + 3], func=AF.Sqrt),
              reads=[("ST", c0 + 2)], writes=[("ST", c0 + 2)])
        P.dve(lambda e, c0=c0: e.reciprocal(out=self.ST[:, c0 + 3:c0 + 4], in_=self.ST[:, c0 + 2:c0 + 3]),
              reads=[("ST", c0 + 2)], writes=[("ST", c0 + 3)])
        for hf in range(2):
            tmp = self.T4[2 + hf]
            ttok = ("T4", 2 + hf)
            P.dve(lambda e, hf=hf, tmp=tmp, s=s, c0=c0: e.scalar_tensor_tensor(
                out=tmp[:], in0=srcs[hf], scalar=self.ST[:, c0 + 3:c0 + 4], in1=self.GROW[:, s, hf * 512:(hf + 1) * 512],
                op0=ALU.mult, op1=ALU.mult), reads=[toks[hf], ("ST", c0 + 3), ("GROW", s)], writes=[ttok])
            P.dve(lambda e, hf=hf, tmp=tmp, t=t: e.tensor_tensor(out=self.H[:, t, hf * 512:(hf + 1) * 512], in0=self.H[:, t, hf * 512:(hf + 1) * 512],
                                                                 in1=tmp[:], op=ALU.add), reads=[ttok, ("H", t)], writes=[("H", t)])

    def in_proj(self):
        P = self.P
        wsrc = self.w_in.rearrange("(k p) n -> p k n", p=128)
        for cb in range(self.ncols // 512):
            wv, wt = self.load_slot(wsrc[:, :, cb * 512:(cb + 1) * 512], [128, KT, 512], ("wb", "w_in"))
            for t in range(NTT):
                bk = t % 4
                for k in range(KT):
                    P.pe(lambda e, wv=wv, k=k, bk=bk, t=t: e.matmul(self.PS[bk][:, :], self.UT[:, k, t * 128:(t + 1) * 128], wv[:, k, :],
                                                                   start=(k == 0), stop=(k == KT - 1)),
                         reads=[wt, ("UT", t)], writes=[self.bank(bk)])
                stg = self.T4[t % 4]
                if t % 2 == 0:
                    P.act(lambda e, stg=stg, bk=bk: e.activation(out=stg[:], in_=self.PS[bk][:, :], func=AF.Copy),
                          reads=[self.bank(bk)], writes=[("T4", t % 4)])
                else:
                    P.dve(lambda e, stg=stg, bk=bk: e.tensor_copy(out=stg[:], in_=self.PS[bk][:, :]), reads=[self.bank(bk)], writes=[("T4", t % 4)])
                self.outs.append(P.dma("sp", lambda e, stg=stg, t=t, cb=cb: e.dma_start(
                    out=self.p_out[t * 128:(t + 1) * 128, cb * 512:(cb + 1) * 512], in_=stg[:]), reads=[("T4", t % 4)]))

    def out_proj(self):
        P = self.P
        nk = self.kdim // 128
        inner = ExitStack()
        OTT = [inner.enter_context(self.nc.sbuf_tensor("OTT%d" % i, [128, nk, 128], BF16)) for i in range(2)]
        otv = self.ot_in.rearrange("(k p) t -> p k t", p=128)
        WOUT = inner.enter_context(self.nc.sbuf_tensor("WOUT", [128, nk, D], BF16))
        P.dma("sp", lambda e: e.dma_start(out=WOUT[:], in_=self.w_out.rearrange("(k p) n -> p k n", p=128)), reads=[("wb", "w_out")], writes=["WOUT"])
        for t in range(NTT):
            ott = OTT[t % 2]
            otok = ("OTT", t % 2)
            P.dma("pool", lambda e, ott=ott, t=t: e.dma_start(out=ott[:], in_=otv[:, :, t * 128:(t + 1) * 128]), writes=[otok])
            b0 = (t % 2) * 2
            for hf in range(2):
                for k in range(nk):
                    P.pe(lambda e, ott=ott, k=k, hf=hf, b0=b0: e.matmul(self.PS[b0 + hf][:, :], ott[:, k, :], WOUT[:, k, hf * 512:(hf + 1) * 512],
                                                                       start=(k == 0), stop=(k == nk - 1)),
                         reads=[otok, "WOUT"], writes=[self.bank(b0 + hf)])
            self.gated_residual(t, [self.PS[b0][:, :], self.PS[b0 + 1][:, :]], [self.bank(b0), self.bank(b0 + 1)])
        inner.close()

    def s5_finish(self):
        P = self.P
        DC = self.sb("DC", [128, KT], F32)
        P.dma("sp", lambda e: e.dma_start(out=DC[:], in_=self.ssm_d.rearrange("o (k p) -> p (o k)", p=128)), writes=["DC"])
        YA = [self.sb("YA%d" % i, [128, 256], F32) for i in range(3)]
        CG = 2.0 * math.sqrt(2.0 / math.pi)
        WH = 256
        for k in range(KT):
            for hb in range(9):
                c0 = hb * WH
                toks = [("UT", t) for t in range(c0 // 128, (c0 + WH) // 128)]
                for i, src in enumerate((self.yf_in, self.yb_in, self.ut_in)):
                    P.dma("sp", lambda e, i=i, src=src, k=k, c0=c0: e.dma_start(out=YA[i][:], in_=src[k * 128:(k + 1) * 128, c0:c0 + WH]),
                          writes=[("YA", i)])
                P.dve(lambda e: e.tensor_tensor(out=YA[0][:], in0=YA[0][:], in1=YA[1][:], op=ALU.add), reads=[("YA", 0), ("YA", 1)], writes=[("YA", 0)])
                P.dve(lambda e, k=k: e.scalar_tensor_tensor(out=YA[0][:], in0=YA[2][:], scalar=DC[:, k:k + 1], in1=YA[0][:], op0=ALU.mult, op1=ALU.add),
                      reads=[("YA", 0), ("YA", 2), "DC"], writes=[("YA", 0)])
                P.act(lambda e: e.activation(out=YA[1][:], in_=YA[0][:], func=AF.Square), reads=[("YA", 0)], writes=[("YA", 1)])
                P.dve(lambda e: e.tensor_scalar(out=YA[1][:], in0=YA[1][:], scalar1=0.044715, scalar2=1.0, op0=ALU.mult, op1=ALU.add),
                      reads=[("YA", 1)], writes=[("YA", 1)])
                P.dve(lambda e: e.tensor_tensor(out=YA[1][:], in0=YA[1][:], in1=YA[0][:], op=ALU.mult), reads=[("YA", 0), ("YA", 1)], writes=[("YA", 1)])
                P.act(lambda e: e.activation(out=YA[1][:], in_=YA[1][:], func=AF.Sigmoid, scale=CG), reads=[("YA", 1)], writes=[("YA", 1)])
                P.dve(lambda e, k=k, c0=c0: e.tensor_tensor(out=self.UT[:, k, c0:c0 + WH], in0=YA[1][:], in1=YA[0][:], op=ALU.mult),
                      reads=[("YA", 0), ("YA", 1)], writes=toks)
        wsrc = self.w_glu.rearrange("(k p) n -> p k n", p=128)
        inner = ExitStack()
        WG = inner.enter_context(self.nc.sbuf_tensor("WG", [128, KT, 2 * D], BF16))
        P.dma("sp", lambda e: e.dma_start(out=WG[:], in_=wsrc), reads=[("wb", "w_glu")], writes=["WG"])
        YX = [inner.enter_context(self.nc.sbuf_tensor("YX%d" % i, [128, 512], F32)) for i in range(2)]
        for t in range(NTT):
            for hf in range(2):
                ba, bb = hf * 2, hf * 2 + 1
                for k in range(KT):
                    P.pe(lambda e, k=k, t=t, hf=hf, ba=ba: e.matmul(self.PS[ba][:, :], self.UT[:, k, t * 128:(t + 1) * 128],
                                                                   WG[:, k, hf * 512:(hf + 1) * 512], start=(k == 0), stop=(k == KT - 1)),
                         reads=["WG", ("UT", t)], writes=[self.bank(ba)])
                for k in range(KT):
                    P.pe(lambda e, k=k, t=t, hf=hf, bb=bb: e.matmul(self.PS[bb][:, :], self.UT[:, k, t * 128:(t + 1) * 128],
                                                                   WG[:, k, D + hf * 512:D + (hf + 1) * 512], start=(k == 0), stop=(k == KT - 1)),
                         reads=["WG", ("UT", t)], writes=[self.bank(bb)])
                P.act(lambda e, bb=bb, hf=hf: e.activation(out=self.T4[hf][:], in_=self.PS[bb][:, :], func=AF.Sigmoid),
                      reads=[self.bank(bb)], writes=[("T4", hf)])
                P.dve(lambda e, ba=ba, hf=hf: e.tensor_tensor(out=YX[hf][:], in0=self.PS[ba][:, :], in1=self.T4[hf][:], op=ALU.mult),
                      reads=[self.bank(ba), ("T4", hf)], writes=[("YX", hf)])
            self.gated_residual(t, [YX[0][:], YX[1][:]], [("YX", 0), ("YX", 1)])
        inner.close()

    def ffn(self):
        P = self.P
        TB = 256
        HID = self.sb("HID", [128, NFT, TB], BF16)
        wg = self.wg.rearrange("(k p) n -> p k n", p=128)
        wu = self.wu.rearrange("(k p) n -> p k n", p=128)
        wd = self.wd.rearrange("(f p) n -> p f n", p=128)
        for blk in range(NT // TB):
            t0 = blk * TB
            toks = [("UT", t) for t in range(t0 // 128, (t0 + TB) // 128)]
            for f0 in range(0, NFT, 4):
                nf = min(4, NFT - f0)
                gv, gtok = self.load_slot(wg[:, :, f0 * 128:(f0 + nf) * 128], [128, KT, nf * 128], ("wb", "ffn_g"))
                uv, utok = self.load_slot(wu[:, :, f0 * 128:(f0 + nf) * 128], [128, KT, nf * 128], ("wb", "ffn_u"))
                for fi in range(nf):
                    f = f0 + fi
                    bg, bu = (f % 2) * 2, (f % 2) * 2 + 1
                    for k in range(KT):
                        P.pe(lambda e, gv=gv, fi=fi, k=k, bg=bg, t0=t0: e.matmul(self.PS[bg][:, 0:TB], gv[:, k, fi * 128:(fi + 1) * 128],
                                                                                self.UT[:, k, t0:t0 + TB], start=(k == 0), stop=(k == KT - 1)),
                             reads=[gtok] + toks, writes=[self.bank(bg)])
                    for k in range(KT):
                        P.pe(lambda e, uv=uv, fi=fi, k=k, bu=bu, t0=t0: e.matmul(self.PS[bu][:, 0:TB], uv[:, k, fi * 128:(fi + 1) * 128],
                                                                                self.UT[:, k, t0:t0 + TB], start=(k == 0), stop=(k == KT - 1)),
                             reads=[utok] + toks, writes=[self.bank(bu)])
                    sg = self.T4[f % 2]
                    P.act(lambda e, sg=sg, bg=bg: e.activation(out=sg[:, 0:TB], in_=self.PS[bg][:, 0:TB], func=AF.Silu),
                          reads=[self.bank(bg)], writes=[("T4", f % 2)])
                    P.dve(lambda e, sg=sg, bu=bu, f=f: e.tensor_tensor(out=HID[:, f, :], in0=self.PS[bu][:, 0:TB], in1=sg[:, 0:TB], op=ALU.mult),
                          reads=[self.bank(bu), ("T4", f % 2)], writes=[("HID", f)])
            for f0 in range(0, NFT, 4):
                nf = min(4, NFT - f0)
                dv, dtok = self.load_slot(wd[:, f0:f0 + nf, :], [128, nf, D], ("wb", "ffn_d"))
                for sub in range(2):
                    for hf in range(2):
                        bk = 4 + sub * 2 + hf
                        for fi in range(nf):
                            f = f0 + fi
                            P.pe(lambda e, dv=dv, fi=fi, f=f, sub=sub, hf=hf, bk=bk: e.matmul(
                                self.PS[bk][:, :], HID[:, f, sub * 128:(sub + 1) * 128], dv[:, fi, hf * 512:(hf + 1) * 512],
                                start=(f == 0), stop=(f == NFT - 1)), reads=[dtok, ("HID", f)], writes=[self.bank(bk)])
            for sub in range(2):
                b0 = 4 + sub * 2
                self.gated_residual(t0 // 128 + sub, [self.PS[b0][:, :], self.PS[b0 + 1][:, :]], [self.bank(b0), self.bank(b0 + 1)])


class RopeMixin:
    def rope_setup(self, quarter):
        P = self.P
        self.fidx = self.din("fidx", [128, 1])
        self.FR = self.sb("FR", [128, 2], F32)
        P.dma("sp", lambda e: e.dma_start(out=self.FR[:, 0:1], in_=self.fidx), writes=["FR0"])
        P.act(lambda e: e.activation(out=self.FR[:, 1:2], in_=self.FR[:, 0:1], func=AF.Exp, scale=-math.log(10000.0) / quarter),
              reads=["FR0"], writes=["FR"])
        self.COS = self.sb("COS", [128, 512], F32)
        self.SIN = self.sb("SIN", [128, 512], F32)
        self.RT = self.sb("RT", [128, 512], F32)
        self.XF = [self.sb("XF%d" % i, [128, 512], F32) for i in range(2)]
        self.XB = [self.sb("XB%d" % i, [128, 512], BF16) for i in range(2)]
        self.RA = self.sb("RA", [128, 512], F32)
        self.RB = self.sb("RB", [128, 512], F32)

    def rope_tables(self, pos_ap, w):
        P = self.P
        P.dma("sp", lambda e: e.dma_start(out=self.COS[:, 0:w], in_=pos_ap), writes=["COS"])
        P.dve(lambda e: e.tensor_scalar(out=self.SIN[:, 0:w], in0=self.COS[:, 0:w], scalar1=self.FR[:, 1:2], scalar2=None, op0=ALU.mult),
              reads=["COS", "FR"], writes=["SIN"])
        P.dve(lambda e: e.tensor_scalar(out=self.COS[:, 0:w], in0=self.SIN[:, 0:w], scalar1=math.pi / 2, scalar2=None, op0=ALU.add),
              reads=["SIN"], writes=["COS"])
        for X, xt in ((self.SIN, "SIN"), (self.COS, "COS")):
            self.range_reduce(X, xt, self.RT, "RT", w)
            P.act(lambda e, X=X: e.activation(out=X[:, 0:w], in_=X[:, 0:w], func=AF.Sin), reads=[xt], writes=[xt])

    def rope_apply(self, src_ap, dst, dtok, w, i, bank, scale=1.0):
        P = self.P
        xf, xb = self.XF[i % 2], self.XB[i % 2]
        ft, bt = ("XF", i % 2), ("XB", i % 2)
        P.dma("sp", lambda e: e.dma_start(out=xf[:, 0:w], in_=src_ap), writes=[ft])
        P.act(lambda e: e.activation(out=xb[:, 0:w], in_=xf[:, 0:w], func=AF.Copy), reads=[ft], writes=[bt])
        P.pe(lambda e: e.matmul(self.PS[bank][:, 0:w], self.CM[:, 1, :], xb[:, 0:w], start=True, stop=True),
             reads=[bt, "CM"], writes=[self.bank(bank)])
        P.dve(lambda e: e.tensor_tensor(out=self.RA[:, 0:w], in0=xf[:, 0:w], in1=self.COS[:, 0:w], op=ALU.mult), reads=[ft, "COS"], writes=["RA"])
        P.dve(lambda e: e.tensor_tensor(out=self.RB[:, 0:w], in0=self.PS[bank][:, 0:w], in1=self.SIN[:, 0:w], op=ALU.mult),
              reads=[self.bank(bank), "SIN"], writes=["RB"])
        P.dve(lambda e: e.tensor_tensor(out=dst, in0=self.RA[:, 0:w], in1=self.RB[:, 0:w], op=ALU.add), reads=["RA", "RB"], writes=[dtok])


QBLK = [(0, 256)] + [(256 + 512 * i, 512) for i in range(16)]


class AttnCore(Base, RopeMixin):
    def __init__(self, lam_init):
        super().__init__()
        self.lam_init = lam_init
        with self.st:
            self.build()

    def build(self):
        P = self.P
        lam_init = self.lam_init
        NP = 2
        self.qT = self.din("qT", [NP, 128, TALL])
        self.kT = self.din("kT", [NP, 128, TALL])
        self.v = self.din("v", [NP, TALL, 128])
        self.pos = self.din("pos", [128, TALL])
        self.cmat = self.din("cmat", [128, 3, 128])
        self.lam_in = self.din("lam", [128, 256])
        self.sub_in = self.din("subln", [128, 1])
        self.o_out = self.dout("oT", [NP, 128, TALL])
        sb = self.sb
        self.CM = sb("CM", [128, 3, 128], BF16)
        QT = sb("QT", [128, TALL], BF16)
        KTt = sb("KT", [128, TALL], BF16)
        V = sb("V", [128, 66, 128], BF16)
        PT = [sb("PT%d" % i, [128, 512], BF16) for i in range(4)]
        T = [sb("T%d" % i, [128, 512], F32) for i in range(4)]
        OS = [sb("OS%d" % i, [128, 512], F32) for i in range(2)]
        LAMRAW = sb("LAMRAW", [128, 4, 64], F32)
        LAMC = sb("LAMC", [128, 8], F32)
        SUBC = sb("SUBC", [128, 2], F32)
        self.psum()
        P.dma("pool", lambda e: e.dma_start(out=self.CM[:], in_=self.cmat), writes=["CM"])
        self.rope_setup(16)
        P.dma("sp", lambda e: e.dma_start(out=LAMRAW[:].rearrange("p a b -> p (a b)"), in_=self.lam_in), writes=["LAMRAW"])
        P.dma("sp", lambda e: e.dma_start(out=SUBC[:, 0:1], in_=self.sub_in), writes=["SUBC"])
        P.dve(lambda e: e.tensor_tensor(out=LAMRAW[:, 0, :], in0=LAMRAW[:, 0, :], in1=LAMRAW[:, 1, :], op=ALU.mult), reads=["LAMRAW"], writes=["LAMRAW"])
        P.dve(lambda e: e.tensor_tensor(out=LAMRAW[:, 2, :], in0=LAMRAW[:, 2, :], in1=LAMRAW[:, 3, :], op=ALU.mult), reads=["LAMRAW"], writes=["LAMRAW"])
        P.dve(lambda e: e.tensor_reduce(out=LAMC[:, 0:1], in_=LAMRAW[:, 0, :], axis=AX.X, op=ALU.add), reads=["LAMRAW"], writes=["LAMC"])
        P.dve(lambda e: e.tensor_reduce(out=LAMC[:, 1:2], in_=LAMRAW[:, 2, :], axis=AX.X, op=ALU.add), reads=["LAMRAW"], writes=["LAMC"])
        P.act(lambda e: e.activation(out=LAMC[:, 2:4], in_=LAMC[:, 0:2], func=AF.Exp), reads=["LAMC"], writes=["LAMC"])
        P.dve(lambda e: e.tensor_tensor(out=LAMC[:, 4:5], in0=LAMC[:, 3:4], in1=LAMC[:, 2:3], op=ALU.subtract), reads=["LAMC"], writes=["LAMC"])
        P.dve(lambda e: e.tensor_scalar(out=LAMC[:, 5:6], in0=LAMC[:, 4:5], scalar1=-lam_init, scalar2=None, op0=ALU.add), reads=["LAMC"], writes=["LAMC"])
        P.dve(lambda e: e.tensor_scalar(out=SUBC[:, 1:2], in0=SUBC[:, 0:1], scalar1=(1.0 - lam_init) * math.sqrt(128.0), scalar2=None, op0=ALU.mult),
              reads=["SUBC"], writes=["SUBC"])
        scale = 0.125
        pt_rr = [0]
        for pr in range(NP):
            ri = 0
            for (t0, w) in QBLK:
                self.rope_tables(self.pos[:, t0:t0 + w], w)
                for src, dstT, nm in ((self.qT, QT, "QT"), (self.kT, KTt, "KT")):
                    self.rope_apply(src[pr, :, t0:t0 + w], dstT[:, t0:t0 + w], (nm, t0), w, ri, ri % 2)
                    ri += 1
            for c in range(3):
                lo, hi = c * 22, (c + 1) * 22
                P.dma("pool", lambda e, lo=lo, hi=hi, pr=pr: e.dma_start(
                    out=V[:, lo:hi, :], in_=self.v[pr, lo * 128:hi * 128, :].rearrange("(t p) d -> p t d", p=128)), writes=[("V", c)])
            for (q0, w) in QBLK:
                nkt = 2 if q0 == 0 else 66
                qtok = ("QT", q0)
                pend = None
                for kt in range(nkt + 1):
                    cur = None
                    if kt < nkt:
                        kb = 0 if kt < 2 else 256 + ((kt - 2) // 4) * 512
                        rt = [("KT", kb), qtok]
                        sb0, sb1 = (kt % 2) * 2, (kt % 2) * 2 + 1
                        P.pe(lambda e, kt=kt, sb0=sb0, q0=q0, w=w: e.matmul(self.PS[sb0][:, 0:w], KTt[0:64, kt * 128:(kt + 1) * 128], QT[0:64, q0:q0 + w],
                                                                           start=True, stop=True), reads=rt, writes=[self.bank(sb0)])
                        P.pe(lambda e, kt=kt, sb1=sb1, q0=q0, w=w: e.matmul(self.PS[sb1][:, 0:w], KTt[64:128, kt * 128:(kt + 1) * 128], QT[64:128, q0:q0 + w],
                                                                           start=True, stop=True), reads=rt, writes=[self.bank(sb1)])
                        pts = []
                        for sbk in (sb0, sb1):
                            pi = pt_rr[0] % 4
                            pt_rr[0] += 1
                            P.act(lambda e, pi=pi, sbk=sbk, w=w: e.activation(out=PT[pi][:, 0:w], in_=self.PS[sbk][:, 0:w], func=AF.Exp, scale=scale),
                                  reads=[self.bank(sbk)], writes=[("PT", pi)])
                            pts.append(pi)
                        cur = (kt, pts)
                    if pend is not None:
                        pk, ppts = pend
                        for c in range(2):
                            P.pe(lambda e, c=c, pi=ppts[c], pk=pk, w=w, nkt=nkt: e.matmul(self.PS[4 + c][:, 0:w], V[:, pk, :], PT[pi][:, 0:w],
                                                                                         start=(pk == 0), stop=(pk == nkt - 1)),
                                 reads=[("V", pk // 22), ("PT", ppts[c])], writes=[self.bank(4 + c)])
                            P.pe(lambda e, c=c, pi=ppts[c], pk=pk, w=w, nkt=nkt: e.matmul(self.PS[6 + c][:, 0:w], self.CM[:, 2, :], PT[pi][:, 0:w],
                                                                                         start=(pk == 0), stop=(pk == nkt - 1)),
                                 reads=[("PT", ppts[c]), "CM"], writes=[self.bank(6 + c)])
                    pend = cur
                for c in range(2):
                    P.dve(lambda e, c=c, w=w: e.reciprocal(out=T[c][:, 0:w], in_=self.PS[6 + c][:, 0:w]), reads=[self.bank(6 + c)], writes=[("T", c)])
                    P.dve(lambda e, c=c, w=w: e.tensor_tensor(out=T[2 + c][:, 0:w], in0=self.PS[4 + c][:, 0:w], in1=T[c][:, 0:w], op=ALU.mult),
                          reads=[self.bank(4 + c), ("T", c)], writes=[("T", 2 + c)])
                P.dve(lambda e, w=w: e.scalar_tensor_tensor(out=T[0][:, 0:w], in0=T[3][:, 0:w], scalar=LAMC[:, 5:6], in1=T[2][:, 0:w],
                                                            op0=ALU.mult, op1=ALU.add), reads=[("T", 2), ("T", 3), "LAMC"], writes=[("T", 0)])
                pi = pt_rr[0] % 4
                pt_rr[0] += 1
                P.act(lambda e, pi=pi, w=w: e.activation(out=PT[pi][:, 0:w], in_=T[0][:, 0:w], func=AF.Square), reads=[("T", 0)], writes=[("PT", pi)])
                P.pe(lambda e, pi=pi, w=w: e.matmul(self.PS[6][:, 0:w], self.CM[:, 2, :], PT[pi][:, 0:w], start=True, stop=True),
                     reads=[("PT", pi), "CM"], writes=[self.bank(6)])
                P.act(lambda e, w=w: e.activation(out=T[1][:, 0:w], in_=self.PS[6][:, 0:w], func=AF.Ln, bias=128.0 * EPS), reads=[self.bank(6)], writes=[("T", 1)])
                P.act(lambda e, w=w: e.activation(out=T[1][:, 0:w], in_=T[1][:, 0:w], func=AF.Exp, scale=-0.5), reads=[("T", 1)], writes=[("T", 1)])
                P.dve(lambda e, w=w: e.tensor_tensor(out=T[2][:, 0:w], in0=T[0][:, 0:w], in1=T[1][:, 0:w], op=ALU.mult), reads=[("T", 0), ("T", 1)], writes=[("T", 2)])
                oi = (q0 // 256) % 2
                P.act(lambda e, w=w, oi=oi: e.activation(out=OS[oi][:, 0:w], in_=T[2][:, 0:w], func=AF.Copy, scale=SUBC[:, 1:2]),
                      reads=[("T", 2), "SUBC"], writes=[("OS", oi)])
                self.outs.append(P.dma("sp", lambda e, w=w, oi=oi, pr=pr, q0=q0: e.dma_start(out=self.o_out[pr, :, q0:q0 + w], in_=OS[oi][:, 0:w]),
                                       reads=[("OS", oi)]))
        self.finish()


class RetCore(Base, RopeMixin):
    def __init__(self):
        super().__init__()
        with self.st:
            self.build()

    def build(self):
        P = self.P
        self.qT = self.din("qT", [256, TALL])
        self.kT = self.din("kT", [256, TALL])
        self.v = self.din("v", [TALL, 512])
        self.gT = self.din("gT", [512, TALL])
        self.pos = self.din("pos", [2, 128, TALL])
        self.cmat = self.din("cmat", [128, 3, 128])
        self.lg_in = self.din("lg", [128, 2])
        self.didx_in = self.din("didx", [128, 512])
        self.o_out = self.dout("oT", [512, TALL])
        sb = self.sb
        self.CM = sb("CM", [128, 3, 128], BF16)
        QT = sb("QT", [128, 2, TALL], BF16)
        KTt = sb("KT", [128, 2, TALL], BF16)
        V = sb("V", [128, 66, 512], BF16)
        DIDX = sb("DIDX", [128, 512], F32)
        LG = sb("LG", [128, 8], F32)
        BC = sb("BC", [128, 8], F32)
        WT = [sb("WT%d" % i, [128, 512], F32) for i in range(2)]
        MK = sb("MK", [128, 512], F32)
        SM = [sb("SM%d" % i, [128, 512], BF16) for i in range(2)]
        SQ = sb("SQ", [128, 512], BF16)
        RS = sb("RS", [128, 512], F32)
        G = sb("G", [128, 4, 512], F32)
        OS = [sb("OS%d" % i, [128, 512], F32) for i in range(2)]
        self.psum()
        P.dma("pool", lambda e: e.dma_start(out=self.CM[:], in_=self.cmat), writes=["CM"])
        P.dma("sp", lambda e: e.dma_start(out=DIDX[:], in_=self.didx_in), writes=["DIDX"])
        P.dma("sp", lambda e: e.dma_start(out=LG[:, 0:2], in_=self.lg_in), writes=["LG"])
        P.act(lambda e: e.activation(out=LG[:, 2:4], in_=LG[:, 0:2], func=AF.Exp, scale=-1.0), reads=["LG"], writes=["LG"])
        P.act(lambda e: e.activation(out=LG[:, 4:6], in_=LG[:, 2:4], func=AF.Ln, bias=1.0), reads=["LG"], writes=["LG"])
        P.dve(lambda e: e.tensor_scalar(out=LG[:, 2:4], in0=LG[:, 4:6], scalar1=-1.0, scalar2=None, op0=ALU.mult), reads=["LG"], writes=["LG"])
        self.rope_setup(64)
        ri = 0
        for (t0, w) in QBLK:
            for sl in range(2):
                self.rope_tables(self.pos[sl, :, t0:t0 + w], w)
                for src, dstT, nm in ((self.qT, QT, "QT"), (self.kT, KTt, "KT")):
                    self.rope_apply(src[sl * 128:(sl + 1) * 128, t0:t0 + w], dstT[:, sl, t0:t0 + w], (nm, t0), w, ri, ri % 2)
                    ri += 1
        for c in range(6):
            lo, hi = c * 11, (c + 1) * 11
            P.dma("pool", lambda e, lo=lo, hi=hi: e.dma_start(out=V[:, lo:hi, :], in_=self.v[lo * 128:hi * 128, :].rearrange("(t p) d -> p t d", p=128)),
                  writes=[("V", c)])
        gtv = self.gT.rearrange("(c p) t -> p c t", p=128)
        smi = [0]
        for (q0, w) in QBLK:
            qctx = q0 < CTX
            nkt = 2 if qctx else 66
            qtok = ("QT", q0)
            P.dma("sp", lambda e, q0=q0, w=w: e.dma_start(out=G[:, :, 0:w], in_=gtv[:, :, q0:q0 + w]), writes=["G"])
            for kt in range(nkt):
                k0 = kt * 128
                kctx = k0 < CTX
                kb = 0 if kt < 2 else 256 + ((kt - 2) // 4) * 512
                sbk = kt % 2
                for sl in range(2):
                    P.pe(lambda e, kt=kt, sl=sl, sbk=sbk, q0=q0, w=w: e.matmul(self.PS[sbk][:, 0:w], KTt[:, sl, kt * 128:(kt + 1) * 128], QT[:, sl, q0:q0 + w],
                                                                              start=(sl == 0), stop=(sl == 1)),
                         reads=[("KT", kb), qtok], writes=[self.bank(sbk)])
                offf = q0 - k0
                adj = TALL if (kctx and not qctx) else (-TALL if (qctx and not kctx) else 0)
                offb = (k0 - q0) + adj
                dirs = []
                if offf + (w - 1) >= 0:
                    dirs.append((0, 1.0, offf, offf - 127 >= 0))
                if offb + 127 >= 0:
                    dirs.append((1, -1.0, offb, offb - (w - 1) >= 0))
                assert dirs
                for di, (d, sgn, off, full) in enumerate(dirs):
                    wt = WT[di]
                    wtok = ("WT", di)
                    if full:
                        P.dve(lambda e, d=d, off=off, di=di: e.tensor_scalar(out=BC[:, di:di + 1], in0=LG[:, 2 + d:3 + d], scalar1=float(off), scalar2=None, op0=ALU.mult),
                              reads=["LG"], writes=[("BC", di)])
                        scol = LG[:, 2:3] if d == 0 else LG[:, 5:6]
                        P.act(lambda e, wt=wt, scol=scol, di=di, w=w: e.activation(out=wt[:, 0:w], in_=DIDX[:, 0:w], func=AF.Exp, scale=scol, bias=BC[:, di:di + 1]),
                              reads=["DIDX", "LG", ("BC", di)], writes=[wtok])
                    else:
                        P.dve(lambda e, wt=wt, sgn=sgn, off=off, w=w: e.tensor_scalar(out=wt[:, 0:w], in0=DIDX[:, 0:w], scalar1=sgn, scalar2=float(off), op0=ALU.mult, op1=ALU.add),
                              reads=["DIDX"], writes=[wtok])
                        P.dve(lambda e, wt=wt, w=w: e.tensor_single_scalar(out=MK[:, 0:w], in_=wt[:, 0:w], scalar=0.0, op=ALU.is_ge), reads=[wtok], writes=["MK"])
                        P.dve(lambda e, wt=wt, w=w: e.tensor_scalar(out=wt[:, 0:w], in0=wt[:, 0:w], scalar1=0.0, scalar2=None, op0=ALU.max), reads=[wtok], writes=[wtok])
                        P.act(lambda e, wt=wt, d=d, w=w: e.activation(out=wt[:, 0:w], in_=wt[:, 0:w], func=AF.Exp, scale=LG[:, 2 + d:3 + d]), reads=[wtok, "LG"], writes=[wtok])
                        P.dve(lambda e, wt=wt, w=w: e.tensor_tensor(out=wt[:, 0:w], in0=wt[:, 0:w], in1=MK[:, 0:w], op=ALU.mult), reads=[wtok, "MK"], writes=[wtok])
                if len(dirs) == 2:
                    P.dve(lambda e, w=w: e.tensor_tensor(out=WT[0][:, 0:w], in0=WT[0][:, 0:w], in1=WT[1][:, 0:w], op=ALU.add),
                          reads=[("WT", 0), ("WT", 1)], writes=[("WT", 0)])
                si = smi[0] % 2
                smi[0] += 1
                P.dve(lambda e, si=si, sbk=sbk, w=w: e.scalar_tensor_tensor(out=SM[si][:, 0:w], in0=self.PS[sbk][:, 0:w], scalar=1.0 / 16.0, in1=WT[0][:, 0:w],
                                                                           op0=ALU.mult, op1=ALU.mult),
                      reads=[self.bank(sbk), ("WT", 0)], writes=[("SM", si)])
                for c in range(4):
                    P.pe(lambda e, c=c, kt=kt, si=si, w=w, nkt=nkt: e.matmul(self.PS[4 + c][:, 0:w], V[:, kt, c * 128:(c + 1) * 128], SM[si][:, 0:w],
                                                                            start=(kt == 0), stop=(kt == nkt - 1)),
                         reads=[("V", kt // 11), ("SM", si)], writes=[self.bank(4 + c)])
            for c in range(4):
                P.act(lambda e, c=c, w=w: e.activation(out=SQ[:, 0:w], in_=self.PS[4 + c][:, 0:w], func=AF.Square), reads=[self.bank(4 + c)], writes=["SQ"])
                P.pe(lambda e, c=c, w=w: e.matmul(self.PS[2][:, 0:w], self.CM[:, 2, :], SQ[:, 0:w], start=(c == 0), stop=(c == 3)),
                     reads=["SQ", "CM"], writes=[self.bank(2)])
            P.act(lambda e, w=w: e.activation(out=RS[:, 0:w], in_=self.PS[2][:, 0:w], func=AF.Ln, scale=1.0 / 512.0, bias=EPS), reads=[self.bank(2)], writes=["RS"])
            P.act(lambda e, w=w: e.activation(out=RS[:, 0:w], in_=RS[:, 0:w], func=AF.Exp, scale=-0.5), reads=["RS"], writes=["RS"])
            P.act(lambda e, w=w: e.activation(out=G[:, :, 0:w], in_=G[:, :, 0:w], func=AF.Silu), reads=["G"], writes=["G"])
            for c in range(4):
                oi = c % 2
                P.dve(lambda e, c=c, oi=oi, w=w: e.tensor_tensor(out=OS[oi][:, 0:w], in0=self.PS[4 + c][:, 0:w], in1=RS[:, 0:w], op=ALU.mult),
                      reads=[self.bank(4 + c), "RS"], writes=[("OS", oi)])
                P.dve(lambda e, c=c, oi=oi, w=w: e.tensor_tensor(out=OS[oi][:, 0:w], in0=OS[oi][:, 0:w], in1=G[:, c, 0:w], op=ALU.mult),
                      reads=[("OS", oi), "G"], writes=[("OS", oi)])
                self.outs.append(P.dma("sp", lambda e, c=c, oi=oi, w=w, q0=q0: e.dma_start(out=self.o_out[c * 128:(c + 1) * 128, q0:q0 + w], in_=OS[oi][:, 0:w]),
                                       reads=[("OS", oi)]))
        self.finish()


class S5Core(Base):
    def __init__(self):
        super().__init__()
        with self.st:
            self.build()

    def _pair_body(self, q, t0, w, u, utok, BZR, BZI, COST, SINT, RB, CAR, W, HB, YS, CPR, CPI):
        P = self.P
        tl, qq = q // 4, q % 4
        b0, b1 = (q % 2) * 2, (q % 2) * 2 + 1
        P.pe(lambda e, tl=tl, qq=qq, b0=b0, u=u, w=w: e.matmul(self.PS[b0][:, 0:w], BZR[:, tl, qq, :], u[:, tl, 0:w], start=True, stop=True),
             reads=["BZR", utok], writes=[self.bank(b0)])
        P.pe(lambda e, tl=tl, qq=qq, b1=b1, u=u, w=w: e.matmul(self.PS[b1][:, 0:w], BZI[:, tl, qq, :], u[:, tl, 0:w], start=True, stop=True),
             reads=["BZI", utok], writes=[self.bank(b1)])
        cs, sn = COST[:, q, 0:w], SINT[:, q, 0:w]
        ct, stt_ = ("COST", q), ("SINT", q)
        wk = lambda i: W[i][:, 0:w]
        wtk = lambda i: ("W", i)
        P.dve(lambda e, b0=b0, cs=cs: e.tensor_tensor(out=wk(0), in0=self.PS[b0][:, 0:w], in1=cs, op=ALU.mult), reads=[self.bank(b0), ct], writes=[wtk(0)])
        P.dve(lambda e, b1=b1, sn=sn: e.tensor_tensor(out=wk(1), in0=self.PS[b1][:, 0:w], in1=sn, op=ALU.mult), reads=[self.bank(b1), stt_], writes=[wtk(1)])
        P.dve(lambda e: e.tensor_tensor(out=wk(0), in0=wk(0), in1=wk(1), op=ALU.add), reads=[wtk(0), wtk(1)], writes=[wtk(0)])
        P.dve(lambda e, b1=b1, cs=cs: e.tensor_tensor(out=wk(2), in0=self.PS[b1][:, 0:w], in1=cs, op=ALU.mult), reads=[self.bank(b1), ct], writes=[wtk(2)])
        P.dve(lambda e, b0=b0, sn=sn: e.tensor_tensor(out=wk(3), in0=self.PS[b0][:, 0:w], in1=sn, op=ALU.mult), reads=[self.bank(b0), stt_], writes=[wtk(3)])
        P.dve(lambda e: e.tensor_tensor(out=wk(2), in0=wk(2), in1=wk(3), op=ALU.subtract), reads=[wtk(2), wtk(3)], writes=[wtk(2)])
        P.dve(lambda e, q=q: e.tensor_tensor_scan(out=wk(4), data0=RB[:, q, 0:w], data1=wk(0), initial=CAR[:, q, 0:1], op0=ALU.mult, op1=ALU.add),
              reads=[("RB", q), wtk(0), "CAR"], writes=[wtk(4)])
        P.dve(lambda e, q=q: e.tensor_tensor_scan(out=wk(5), data0=RB[:, q, 0:w], data1=wk(2), initial=CAR[:, q, 1:2], op0=ALU.mult, op1=ALU.add),
              reads=[("RB", q), wtk(2), "CAR"], writes=[wtk(5)])
        P.dve(lambda e, cs=cs: e.tensor_tensor(out=wk(0), in0=wk(4), in1=cs, op=ALU.mult), reads=[wtk(4), ct], writes=[wtk(0)])
        P.dve(lambda e, sn=sn: e.tensor_tensor(out=wk(1), in0=wk(5), in1=sn, op=ALU.mult), reads=[wtk(5), stt_], writes=[wtk(1)])
        P.dve(lambda e: e.tensor_tensor(out=wk(6), in0=wk(0), in1=wk(1), op=ALU.subtract), reads=[wtk(0), wtk(1)], writes=[wtk(6)])
        P.dve(lambda e, sn=sn: e.tensor_tensor(out=wk(2), in0=wk(4), in1=sn, op=ALU.mult), reads=[wtk(4), stt_], writes=[wtk(2)])
        P.dve(lambda e, cs=cs: e.tensor_tensor(out=wk(3), in0=wk(5), in1=cs, op=ALU.mult), reads=[wtk(5), ct], writes=[wtk(3)])
        P.dve(lambda e: e.tensor_tensor(out=wk(7), in0=wk(2), in1=wk(3), op=ALU.add), reads=[wtk(2), wtk(3)], writes=[wtk(7)])
        P.dve(lambda e, q=q, w=w: e.tensor_copy(out=CAR[:, q, 0:1], in_=W[6][:, w - 1:w]), reads=[wtk(6)], writes=["CAR"])
        P.dve(lambda e, q=q, w=w: e.tensor_copy(out=CAR[:, q, 1:2], in_=W[7][:, w - 1:w]), reads=[wtk(7)], writes=["CAR"])
        hr, hi = HB[(q % 2) * 2], HB[(q % 2) * 2 + 1]
        hrt, hit = ("HB", (q % 2) * 2), ("HB", (q % 2) * 2 + 1)
        P.act(lambda e, hr=hr, w=w: e.activation(out=hr[:, 0:w], in_=W[6][:, 0:w], func=AF.Copy), reads=[wtk(6)], writes=[hrt])
        P.act(lambda e, hi=hi, w=w: e.activation(out=hi[:, 0:w], in_=W[7][:, 0:w], func=AF.Copy), reads=[wtk(7)], writes=[hit])
        yb = 4 + tl
        P.pe(lambda e, q=q, hr=hr, yb=yb, qq=qq, w=w: e.matmul(self.PS[yb][:, 0:w], CPR[:, q, :], hr[:, 0:w], start=(qq == 0), stop=False),
             reads=["CPR", hrt], writes=[self.bank(yb)])
        P.pe(lambda e, q=q, hi=hi, yb=yb, qq=qq, w=w: e.matmul(self.PS[yb][:, 0:w], CPI[:, q, :], hi[:, 0:w], start=False, stop=(qq == 3)),
             reads=["CPI", hit], writes=[self.bank(yb)])
        if qq == 3:
            ys = YS[tl % 2]
            P.act(lambda e, ys=ys, yb=yb, w=w: e.activation(out=ys[:, 0:w], in_=self.PS[yb][:, 0:w], func=AF.Copy), reads=[self.bank(yb)], writes=[("YS", tl % 2)])
            self.outs.append(P.dma("sp", lambda e, ys=ys, tl=tl, t0=t0, w=w: e.dma_start(out=self.y_out[tl * 128:(tl + 1) * 128, t0:t0 + w], in_=ys[:, 0:w]),
                                   reads=[("YS", tl % 2)]))

    def build(self):
        P = self.P
        NPAIR = 16
        self.uT = self.din("uT", [512, TALL])
        self.bzr = self.din("bzr", [128, 4, 4, 128])
        self.bzi = self.din("bzi", [128, 4, 4, 128])
        self.czr = self.din("czr", [128, NPAIR, 128])
        self.czi = self.din("czi", [128, NPAIR, 128])
        self.lre = self.din("lre", [128, NPAIR])
        self.lim = self.din("lim", [128, NPAIR])
        self.ldt = self.din("ldt", [128, NPAIR])
        self.sidx = self.din("sidx", [128, 512])
        self.y_out = self.dout("yT", [512, TALL])
        sb = self.sb
        BZR = sb("BZR", [128, 4, 4, 128], BF16)
        BZI = sb("BZI", [128, 4, 4, 128], BF16)
        CR = sb("CR", [128, NPAIR, 128], F32)
        CI = sb("CI", [128, NPAIR, 128], F32)
        CPR = sb("CPR", [128, NPAIR, 128], BF16)
        CPI = sb("CPI", [128, NPAIR, 128], BF16)
        SIDX = sb("SIDX", [128, 512], F32)
        COST = sb("COST", [128, NPAIR, 512], F32)
        SINT = sb("SINT", [128, NPAIR, 512], F32)
        RB = sb("RB", [128, NPAIR, 512], F32)
        PR = sb("PR", [128, 16, NPAIR], F32)
        CAR = sb("CAR", [128, NPAIR, 2], F32)
        U = [sb("U%d" % i, [128, 4, 512], BF16) for i in range(2)]
        W = [sb("W%d" % i, [128, 512], F32) for i in range(8)]
        HB = [sb("HB%d" % i, [128, 512], BF16) for i in range(4)]
        YS = [sb("YS%d" % i, [128, 512], F32) for i in range(2)]
        TMP = sb("TMP", [128, 512], F32)
        self.psum()
        P.dma("pool", lambda e: e.dma_start(out=BZR[:], in_=self.bzr), writes=["BZR"])
        P.dma("pool", lambda e: e.dma_start(out=BZI[:], in_=self.bzi), writes=["BZI"])
        P.dma("sp", lambda e: e.dma_start(out=CR[:], in_=self.czr), writes=["CR"])
        P.dma("sp", lambda e: e.dma_start(out=CI[:], in_=self.czi), writes=["CI"])
        P.dma("sp", lambda e: e.dma_start(out=SIDX[:], in_=self.sidx), writes=["SIDX"])
        for i, src in enumerate((self.lre, self.lim, self.ldt)):
            P.dma("sp", lambda e, i=i, src=src: e.dma_start(out=PR[:, i, :], in_=src), writes=[("PR", i)])
        pr = lambda i: PR[:, i, :]
        pt = lambda i: ("PR", i)
        P.act(lambda e: e.activation(out=pr(3), in_=pr(2), func=AF.Exp), reads=[pt(2)], writes=[pt(3)])
        P.dve(lambda e: e.tensor_tensor(out=pr(4), in0=pr(0), in1=pr(3), op=ALU.mult), reads=[pt(0), pt(3)], writes=[pt(4)])
        P.dve(lambda e: e.tensor_tensor(out=pr(5), in0=pr(1), in1=pr(3), op=ALU.mult), reads=[pt(1), pt(3)], writes=[pt(5)])
        P.act(lambda e: e.activation(out=pr(6), in_=pr(4), func=AF.Exp), reads=[pt(4)], writes=[pt(6)])
        P.dve(lambda e: e.tensor_copy(out=pr(7), in_=pr(5)), reads=[pt(5)], writes=[pt(7)])
        P.dve(lambda e: e.tensor_scalar(out=pr(8), in0=pr(5), scalar1=math.pi / 2, scalar2=None, op0=ALU.add), reads=[pt(5)], writes=[pt(8)])
        for i in (7, 8):
            self.range_reduce(PR[:, i, :], pt(i), TMP, "TMP", NPAIR)
            P.act(lambda e, i=i: e.activation(out=pr(i), in_=pr(i), func=AF.Sin), reads=[pt(i)], writes=[pt(i)])
        P.dve(lambda e: e.tensor_tensor(out=pr(9), in0=pr(6), in1=pr(8), op=ALU.mult), reads=[pt(6), pt(8)], writes=[pt(9)])
        P.dve(lambda e: e.tensor_scalar(out=pr(9), in0=pr(9), scalar1=-1.0, scalar2=None, op0=ALU.add), reads=[pt(9)], writes=[pt(9)])
        P.dve(lambda e: e.tensor_tensor(out=pr(10), in0=pr(6), in1=pr(7), op=ALU.mult), reads=[pt(6), pt(7)], writes=[pt(10)])
        P.dve(lambda e: e.tensor_tensor(out=pr(11), in0=pr(0), in1=pr(0), op=ALU.mult), reads=[pt(0)], writes=[pt(11)])
        P.dve(lambda e: e.tensor_tensor(out=pr(12), in0=pr(1), in1=pr(1), op=ALU.mult), reads=[pt(1)], writes=[pt(12)])
        P.dve(lambda e: e.tensor_tensor(out=pr(11), in0=pr(11), in1=pr(12), op=ALU.add), reads=[pt(11), pt(12)], writes=[pt(11)])
        P.dve(lambda e: e.reciprocal(out=pr(11), in_=pr(11)), reads=[pt(11)], writes=[pt(11)])
        P.dve(lambda e: e.tensor_tensor(out=pr(12), in0=pr(9), in1=pr(0), op=ALU.mult), reads=[pt(9), pt(0)], writes=[pt(12)])
        P.dve(lambda e: e.tensor_tensor(out=pr(13), in0=pr(10), in1=pr(1), op=ALU.mult), reads=[pt(10), pt(1)], writes=[pt(13)])
        P.dve(lambda e: e.tensor_tensor(out=pr(12), in0=pr(12), in1=pr(13), op=ALU.add), reads=[pt(12), pt(13)], writes=[pt(12)])
        P.dve(lambda e: e.tensor_tensor(out=pr(12), in0=pr(12), in1=pr(11), op=ALU.mult), reads=[pt(12), pt(11)], writes=[pt(12)])
        P.dve(lambda e: e.tensor_tensor(out=pr(13), in0=pr(10), in1=pr(0), op=ALU.mult), reads=[pt(10), pt(0)], writes=[pt(13)])
        P.dve(lambda e: e.tensor_tensor(out=pr(14), in0=pr(9), in1=pr(1), op=ALU.mult), reads=[pt(9), pt(1)], writes=[pt(14)])
        P.dve(lambda e: e.tensor_tensor(out=pr(13), in0=pr(13), in1=pr(14), op=ALU.subtract), reads=[pt(13), pt(14)], writes=[pt(13)])
        P.dve(lambda e: e.tensor_tensor(out=pr(13), in0=pr(13), in1=pr(11), op=ALU.mult), reads=[pt(13), pt(11)], writes=[pt(13)])
        P.dve(lambda e: e.tensor_scalar(out=pr(14), in0=pr(12), scalar1=-1.0, scalar2=None, op0=ALU.mult), reads=[pt(12)], writes=[pt(14)])
        P.dve(lambda e: e.tensor_scalar(out=pr(15), in0=pr(13), scalar1=-1.0, scalar2=None, op0=ALU.mult), reads=[pt(13)], writes=[pt(15)])
        alltok = [pt(i) for i in range(16)]
        for q in range(NPAIR):
            P.dve(lambda e, q=q: e.tensor_scalar(out=W[0][:, 0:128], in0=CR[:, q, :], scalar1=PR[:, 12, q:q + 1], scalar2=None, op0=ALU.mult),
                  reads=["CR"] + alltok, writes=[("W", 0)])
            P.dve(lambda e, q=q: e.scalar_tensor_tensor(out=CPR[:, q, :], in0=CI[:, q, :], scalar=PR[:, 15, q:q + 1], in1=W[0][:, 0:128], op0=ALU.mult, op1=ALU.add),
                  reads=["CI", ("W", 0)] + alltok, writes=["CPR"])
            P.dve(lambda e, q=q: e.tensor_scalar(out=W[1][:, 0:128], in0=CR[:, q, :], scalar1=PR[:, 15, q:q + 1], scalar2=None, op0=ALU.mult),
                  reads=["CR"] + alltok, writes=[("W", 1)])
            P.dve(lambda e, q=q: e.scalar_tensor_tensor(out=CPI[:, q, :], in0=CI[:, q, :], scalar=PR[:, 14, q:q + 1], in1=W[1][:, 0:128], op0=ALU.mult, op1=ALU.add),
                  reads=["CI", ("W", 1)] + alltok, writes=["CPI"])
            P.dve(lambda e, q=q: e.tensor_scalar(out=SINT[:, q, :], in0=SIDX[:], scalar1=PR[:, 5, q:q + 1], scalar2=None, op0=ALU.mult),
                  reads=["SIDX"] + alltok, writes=[("SINT", q)])
            P.dve(lambda e, q=q: e.tensor_scalar(out=COST[:, q, :], in0=SINT[:, q, :], scalar1=math.pi / 2, scalar2=None, op0=ALU.add),
                  reads=[("SINT", q)], writes=[("COST", q)])
            for X, nm in ((SINT, "SINT"), (COST, "COST")):
                self.range_reduce(X[:, q, :], (nm, q), TMP, "TMP", 512)
                P.act(lambda e, X=X, q=q: e.activation(out=X[:, q, :], in_=X[:, q, :], func=AF.Sin), reads=[(nm, q)], writes=[(nm, q)])
            P.dve(lambda e, q=q: e.tensor_copy(out=RB[:, q, :], in_=PR[:, 6, q:q + 1].to_broadcast([128, 512])), reads=alltok, writes=[("RB", q)])
        P.dve(lambda e: e.memset(CAR[:].rearrange("p a b -> p (a b)"), 0.0), writes=["CAR"])
        uv = self.uT.rearrange("(t p) s -> p t s", p=128)
        for bi, (t0, w) in enumerate(QBLK):
            u = U[bi % 2]
            utok = ("U", bi % 2)
            P.dma("pool", lambda e, u=u, t0=t0, w=w: e.dma_start(out=u[:, :, 0:w], in_=uv[:, :, t0:t0 + w]), writes=[utok])
            for q in range(NPAIR):
                self._pair_body(q, t0, w, u, utok, BZR, BZI, COST, SINT, RB, CAR, W, HB, YS, CPR, CPI)
        self.finish()


def cmat_const(rot_half):
    cm = np.zeros((128, 3, 128), np.float32)
    cm[:, 0, :] = np.eye(128, dtype=np.float32)
    rot = np.zeros((128, 128), np.float32)
    for base in range(0, 128, 2 * rot_half):
        for i in range(rot_half):
            rot[base + rot_half + i, base + i] = -1.0
            rot[base + i, base + rot_half + i] = 1.0
    cm[:, 1, :] = rot
    cm[:, 2, :] = 1.0
    return cm


def pos_table_attn():
    pos = np.zeros((128, TALL), np.float32)
    tok = np.arange(SEQ)
    row, col = (tok // 64).astype(np.float32), (tok % 64).astype(np.float32)
    for p in range(128):
        pos[p, CTX:] = row if (p % 64) < 32 else col
    return pos


_PROGS = {}
VERBOSE = False


def get_prog(key, ctor):
    if key not in _PROGS:
        _PROGS[key] = ctor().nc
    return _PROGS[key]


def launch(nc, in_maps):
    return run_bass_kernel_spmd(nc, in_maps, core_ids=list(range(8))).results


def shard8(w2d, r, flat_cols=None):
    return np.ascontiguousarray(w2d)


def tok_common(inp, l, r, h, hc):
    b, j = r // 4, r % 4
    return {
        "h_loc": np.ascontiguousarray(np.concatenate([hc[b], h[b, j * LAT:(j + 1) * LAT]], axis=0)),
        "cvec": np.stack([inp["c"][b], inp["c_ctx"]], axis=0).astype(np.float32),
        "mod_b": np.ascontiguousarray(inp["mod_b"][l][None, :]),
        "norm_g": np.ascontiguousarray(inp["norm_g"][l]),
        "cmat": cmat_const(16),
        "mod_w": shard8(inp["mod_w"][l], r),
    }


def ffn_weights(inp, l, r):
    return {"ffn_g": shard8(inp["ffn_w_gate"][l], r, 1024), "ffn_u": shard8(inp["ffn_w_up"][l], r, 1024), "ffn_d": shard8(inp["ffn_w_down"][l], r)}


def split_h(res, key="h_out"):
    h = np.zeros((2, SEQ, D), np.float32)
    hc = np.zeros((2, CTX, D), np.float32)
    for r in range(8):
        b, j = r // 4, r % 4
        o = res[r][key]
        h[b, j * LAT:(j + 1) * LAT] = o[CTX:]
        if j == 0:
            hc[b] = o[:CTX]
    return h, hc


def gather_tokens(res, key, ncols):
    out = np.zeros((2, TALL, ncols), np.float32)
    for r in range(8):
        b, j = r // 4, r % 4
        o = res[r][key]
        out[b, CTX + j * LAT: CTX + (j + 1) * LAT] = o[CTX:]
        if j == 0:
            out[b, :CTX] = o[:CTX]
    return out


def scatter_featmajor(full, r):
    b, j = r // 4, r % 4
    return np.ascontiguousarray(np.concatenate([full[b][:, :CTX], full[b][:, CTX + j * LAT: CTX + (j + 1) * LAT]], axis=1))


def attention_layer(inp, l, h, hc):
    jx = l // 3
    lam_init = 0.8 - 0.6 * math.exp(-0.3 * l)
    ncA = get_prog(("A", 3072), lambda: TokProg("A", ncols=3072))
    maps = []
    for r in range(8):
        m = tok_common(inp, l, r, h, hc)
        m["w_in"] = shard8(inp["attn_w_in"][jx], r)
        maps.append(m)
    p = gather_tokens(launch(ncA, maps), "p_out", 3072)
    ncB = get_prog(("attn", round(lam_init, 6)), lambda: AttnCore(lam_init))
    pos = pos_table_attn()
    cm = cmat_const(16)
    fidx = (np.arange(128) % 16).astype(np.float32)[:, None]
    lam_rep = np.ascontiguousarray(np.broadcast_to(inp["attn_lambda"][jx].reshape(1, 256), (128, 256))).astype(np.float32)
    sub = np.ascontiguousarray(inp["attn_subln"][jx].reshape(128, 1)).astype(np.float32)
    maps = []
    for r in range(8):
        qs, ks, vs = [], [], []
        for pr in range(2):
            idx = r * 2 + pr
            b, hd = idx // 8, idx % 8
            qs.append(p[b][:, hd * 128:(hd + 1) * 128].T)
            ks.append(p[b][:, 1024 + hd * 128: 1024 + (hd + 1) * 128].T)
            vs.append(p[b][:, 2048 + hd * 128: 2048 + (hd + 1) * 128])
        maps.append({"qT": np.ascontiguousarray(np.stack(qs)), "kT": np.ascontiguousarray(np.stack(ks)), "v": np.ascontiguousarray(np.stack(vs)),
                     "pos": pos, "cmat": cm, "fidx": fidx, "lam": lam_rep, "subln": sub})
    resB = launch(ncB, maps)
    oT = [np.zeros((1024, TALL), np.float32) for _ in range(2)]
    for r in range(8):
        for pr in range(2):
            idx = r * 2 + pr
            b, hd = idx // 8, idx % 8
            oT[b][hd * 128:(hd + 1) * 128] = resB[r]["oT"][pr]
    ncC = get_prog(("C", 1024), lambda: TokProg("C", kdim=1024))
    maps = []
    for r in range(8):
        m = tok_common(inp, l, r, h, hc)
        m.update(ffn_weights(inp, l, r))
        m["w_out"] = shard8(inp["attn_w_out"][jx], r)
        m["ot_in"] = scatter_featmajor(oT, r)
        maps.append(m)
    return split_h(launch(ncC, maps))


def pos_table_ret():
    pos = np.zeros((2, 128, TALL), np.float32)
    tok = np.arange(SEQ)
    pos[0, :, CTX:] = (tok // 64).astype(np.float32)[None, :]
    pos[1, :, CTX:] = (tok % 64).astype(np.float32)[None, :]
    return pos


def retention_layer(inp, l, h, hc):
    ncA = get_prog(("A", 6144), lambda: TokProg("A", ncols=6144))
    maps = []
    for r in range(8):
        m = tok_common(inp, l, r, h, hc)
        m["w_in"] = shard8(inp["ret_w_in"][0], r)
        maps.append(m)
    p = gather_tokens(launch(ncA, maps), "p_out", 6144)
    ncB = get_prog(("ret",), lambda: RetCore())
    pos = pos_table_ret()
    cm = cmat_const(64)
    fidx = (np.arange(128) % 64).astype(np.float32)[:, None]
    didx = (np.arange(512)[None, :] - np.arange(128)[:, None]).astype(np.float32)
    maps = []
    for r in range(8):
        b, hd = r // 4, r % 4
        lg = np.ascontiguousarray(np.broadcast_to(inp["ret_decay_logit"][0][:, hd].reshape(1, 2), (128, 2))).astype(np.float32)
        maps.append({"qT": np.ascontiguousarray(p[b][:, hd * 256:(hd + 1) * 256].T),
                     "kT": np.ascontiguousarray(p[b][:, 1024 + hd * 256: 1024 + (hd + 1) * 256].T),
                     "v": np.ascontiguousarray(p[b][:, 2048 + hd * 512: 2048 + (hd + 1) * 512]),
                     "gT": np.ascontiguousarray(p[b][:, 4096 + hd * 512: 4096 + (hd + 1) * 512].T),
                     "pos": pos, "cmat": cm, "fidx": fidx, "lg": lg, "didx": didx})
    resB = launch(ncB, maps)
    oT = [np.zeros((2048, TALL), np.float32) for _ in range(2)]
    for r in range(8):
        b, hd = r // 4, r % 4
        oT[b][hd * 512:(hd + 1) * 512] = resB[r]["oT"]
    ncC = get_prog(("C", 2048), lambda: TokProg("C", kdim=2048))
    maps = []
    for r in range(8):
        m = tok_common(inp, l, r, h, hc)
        m.update(ffn_weights(inp, l, r))
        m["w_out"] = shard8(inp["ret_w_out"][0], r)
        m["ot_in"] = scatter_featmajor(oT, r)
        maps.append(m)
    return split_h(launch(ncC, maps))


def flip_seq(a):
    return np.ascontiguousarray(np.concatenate([a[..., :CTX][..., ::-1], a[..., CTX:][..., ::-1]], axis=-1))


def s5_layer(inp, l, h, hc):
    ncA = get_prog(("A_s5",), lambda: TokProg("A_s5"))
    maps = [tok_common(inp, l, r, h, hc) for r in range(8)]
    resA = launch(ncA, maps)
    uT = [np.zeros((D, TALL), np.float32) for _ in range(2)]
    for r in range(8):
        b, j = r // 4, r % 4
        o = resA[r]["ut_out"]
        uT[b][:, CTX + j * LAT: CTX + (j + 1) * LAT] = o[:, CTX:]
        if j == 0:
            uT[b][:, :CTX] = o[:, :CTX]
    ncB = get_prog(("s5",), lambda: S5Core())
    sidx = np.ascontiguousarray(np.broadcast_to(np.arange(1, 513, dtype=np.float32)[None, :], (128, 512)))
    maps = []
    for r in range(8):
        b, d, gh = r // 4, (r % 4) // 2, r % 2
        u = uT[b][gh * 512:(gh + 1) * 512]
        if d == 1:
            u = flip_seq(u)
        bzr = np.zeros((128, 4, 4, 128), np.float32)
        bzi = np.zeros((128, 4, 4, 128), np.float32)
        czr = np.zeros((128, 16, 128), np.float32)
        czi = np.zeros((128, 16, 128), np.float32)
        lre = np.zeros((128, 16), np.float32)
        lim = np.zeros((128, 16), np.float32)
        ldt = np.zeros((128, 16), np.float32)
        for q in range(16):
            tl, qq = q // 4, q % 4
            for gs in range(2):
                g = gh * 32 + 2 * q + gs
                rows = slice(32 * qq + 16 * gs, 32 * qq + 16 * gs + 16)
                st = slice(64 * gs, 64 * gs + 64)
                bzr[rows, tl, qq, st] = inp["ssm_b_re"][0, d, g].T
                bzi[rows, tl, qq, st] = inp["ssm_b_im"][0, d, g].T
                czr[st, q, rows] = inp["ssm_c_re"][0, d, g].T
                czi[st, q, rows] = inp["ssm_c_im"][0, d, g].T
                lre[st, q] = inp["ssm_lambda_re"][0, d, g]
                lim[st, q] = inp["ssm_lambda_im"][0, d, g]
                ldt[st, q] = inp["ssm_log_dt"][0, d, g]
        maps.append({"uT": np.ascontiguousarray(u), "bzr": bzr, "bzi": bzi, "czr": czr, "czi": czi, "lre": lre, "lim": lim, "ldt": ldt, "sidx": sidx})
    resB = launch(ncB, maps)
    ydir = [[np.zeros((D, TALL), np.float32) for _ in range(2)] for _ in range(2)]
    for r in range(8):
        b, d, gh = r // 4, (r % 4) // 2, r % 2
        y = resB[r]["yT"]
        if d == 1:
            y = flip_seq(y)
        ydir[d][b][gh * 512:(gh + 1) * 512] = y
    ncC = get_prog(("C_s5",), lambda: TokProg("C_s5"))
    maps = []
    for r in range(8):
        m = tok_common(inp, l, r, h, hc)
        m.update(ffn_weights(inp, l, r))
        m["w_glu"] = shard8(inp["ssm_w_glu"][0], r)
        m["ssm_d"] = np.ascontiguousarray(inp["ssm_d"][0][None, :])
        m["yf_in"] = scatter_featmajor(ydir[0], r)
        m["yb_in"] = scatter_featmajor(ydir[1], r)
        m["ut_in"] = scatter_featmajor(uT, r)
        maps.append(m)
    return split_h(launch(ncC, maps))


def kernel(**inputs):
    inp = {k: np.asarray(v, dtype=np.float32) for k, v in inputs.items()}
    h, hc = inp["x"], inp["ctx"]
    for l in range(DEPTH):
        kind = l % 3
        if kind == 0:
            h, hc = attention_layer(inp, l, h, hc)
        elif kind == 1:
            h, hc = retention_layer(inp, l, h, hc)
        else:
            h, hc = s5_layer(inp, l, h, hc)
    return np.ascontiguousarray(h.astype(np.float32))
```
